# Optimizing a Trainium2 kernel written in Bass

```python
import math
import jax, jax.numpy as jnp
from jax import lax
import numpy as np

D_MODEL = 1024
BATCH = 4
SEQ = 4096
DEPTH = 1

HEAD_DIM = 64
N_HEADS = D_MODEL // HEAD_DIM
N_DIFF_HEADS = N_HEADS // 4
DIFF_WIDTH = N_DIFF_HEADS * 2 * HEAD_DIM
N_MOBA_HEADS = N_HEADS // 2
MOBA_WIDTH = N_MOBA_HEADS * HEAD_DIM
MIX_WIDTH = DIFF_WIDTH + MOBA_WIDTH
ROT_DIM = HEAD_DIM // 4
ROPE_THETA = 500000.0
MOBA_BLOCK = 256
MOBA_TOPK = 3
MOBA_Q_CHUNK = 32
DIFF_Q_BLOCK = 128
D_FF = -(-8 * D_MODEL // (3 * 256)) * 256
ALPHA = (2.0 * DEPTH) ** 0.25
BETA = (8.0 * DEPTH) ** -0.25
LN_EPS = 1e-5

kernel_name = "hymba_diffattn_moba_deepnorm"


def lambda_init_for(layer_idx):
    return 0.8 - 0.6 * math.exp(-0.3 * layer_idx)


def rope_tables(seq):
    inv = ROPE_THETA ** (-jnp.arange(0, ROT_DIM, 2, dtype=jnp.float32) / ROT_DIM)
    ang = jnp.arange(seq, dtype=jnp.float32)[:, None] * inv[None, :]
    return jnp.cos(ang), jnp.sin(ang)


def partial_rope(x, cos, sin):
    half = ROT_DIM // 2
    x1 = x[..., :half].astype(jnp.float32)
    x2 = x[..., half:ROT_DIM].astype(jnp.float32)
    rot = jnp.concatenate([x1 * cos - x2 * sin, x2 * cos + x1 * sin], axis=-1).astype(x.dtype)
    return jnp.concatenate([rot, x[..., ROT_DIM:]], axis=-1)


def layer_norm(x, g, b):
    xf = x.astype(jnp.float32)
    mu = jnp.mean(xf, axis=-1, keepdims=True)
    var = jnp.mean(jnp.square(xf - mu), axis=-1, keepdims=True)
    return ((xf - mu) * lax.rsqrt(var + LN_EPS) * g.astype(jnp.float32) + b.astype(jnp.float32)).astype(x.dtype)


def diff_attention(q, k, v, lam_params, subln_g, lambda_init, cos, sin):
    B, S, _ = q.shape
    H, dh = N_DIFF_HEADS, HEAD_DIM
    q = q.reshape(B, S, H, 2, dh).transpose(0, 2, 3, 1, 4)
    k = k.reshape(B, S, H, 2, dh).transpose(0, 2, 3, 1, 4)
    v = v.reshape(B, S, H, 2 * dh).transpose(0, 2, 1, 3)
    q = partial_rope(q, cos, sin)
    k = partial_rope(k, cos, sin)
    lp = lam_params.astype(jnp.float32)
    lam = jnp.exp(jnp.sum(lp[0] * lp[1])) - jnp.exp(jnp.sum(lp[2] * lp[3])) + lambda_init
    scale = dh ** -0.5
    nblk = S // DIFF_Q_BLOCK
    qb = q.reshape(B, H, 2, nblk, DIFF_Q_BLOCK, dh).transpose(3, 0, 1, 2, 4, 5)
    kpos = jnp.arange(S)

    def one_block(args):
        qblk, i = args
        qpos = i * DIFF_Q_BLOCK + jnp.arange(DIFF_Q_BLOCK)
        s = jnp.einsum('bhcqd,bhckd->bhcqk', qblk, k).astype(jnp.float32) * scale
        s = jnp.where(kpos[None, :] <= qpos[:, None], s, -jnp.inf)
        p = jax.nn.softmax(s, axis=-1)
        a = p[:, :, 0] - lam * p[:, :, 1]
        return jnp.einsum('bhqk,bhkd->bhqd', a.astype(v.dtype), v)

    o = lax.map(one_block, (qb, jnp.arange(nblk)))
    o = o.transpose(1, 2, 0, 3, 4).reshape(B, H, S, 2 * dh)
    of = o.astype(jnp.float32)
    of = of * lax.rsqrt(jnp.mean(jnp.square(of), axis=-1, keepdims=True) + LN_EPS)
    of = of * subln_g.astype(jnp.float32) * (1.0 - lambda_init)
    return of.astype(v.dtype).transpose(0, 2, 1, 3).reshape(B, S, DIFF_WIDTH)


def moba_attention(q, k, v, cos, sin):
    B, S, _ = q.shape
    H, dh, BS, QC = N_MOBA_HEADS, HEAD_DIM, MOBA_BLOCK, MOBA_Q_CHUNK
    q = partial_rope(q.reshape(B, S, H, dh).transpose(0, 2, 1, 3), cos, sin)
    k = partial_rope(k.reshape(B, S, H, dh).transpose(0, 2, 1, 3), cos, sin)
    v = v.reshape(B, S, H, dh).transpose(0, 2, 1, 3)
    nb = -(-S // BS)
    pad = nb * BS - S
    k = jnp.pad(k, ((0, 0), (0, 0), (0, pad), (0, 0)))
    v = jnp.pad(v, ((0, 0), (0, 0), (0, pad), (0, 0)))
    kb = k.reshape(B, H, nb, BS, dh)
    vb = v.reshape(B, H, nb, BS, dh)
    kmean = jnp.mean(kb.astype(jnp.float32), axis=3).astype(k.dtype)
    topk = min(MOBA_TOPK, nb)
    scale = dh ** -0.5
    nc = S // QC
    qc = q.reshape(B, H, nc, QC, dh).transpose(2, 0, 1, 3, 4)
    blk_ids = jnp.arange(nb)
    gather = jax.vmap(jax.vmap(lambda blocks, idx: blocks[idx]))

    def one_chunk(args):
        qch, c = args
        start = c * QC
        qpos = start + jnp.arange(QC)
        own = start // BS
        gate = jnp.einsum('bhqd,bhnd->bhqn', qch, kmean).astype(jnp.float32)
        gate = jnp.where(blk_ids < own, gate, -jnp.inf)
        _, idx = lax.top_k(gate, topk)
        valid = jnp.arange(topk) < own
        k_sel = gather(kb, idx)
        v_sel = gather(vb, idx)
        s_sel = jnp.einsum('bhqd,bhqjkd->bhqjk', qch, k_sel).astype(jnp.float32) * scale
        s_sel = jnp.where(valid[:, None], s_sel, -jnp.inf).reshape(B, H, QC, topk * BS)
        k_own = lax.dynamic_index_in_dim(kb, own, axis=2, keepdims=False)
        v_own = lax.dynamic_index_in_dim(vb, own, axis=2, keepdims=False)
        kpos_own = own * BS + jnp.arange(BS)
        s_own = jnp.einsum('bhqd,bhkd->bhqk', qch, k_own).astype(jnp.float32) * scale
        s_own = jnp.where(kpos_own[None, :] <= qpos[:, None], s_own, -jnp.inf)
        p = jax.nn.softmax(jnp.concatenate([s_sel, s_own], axis=-1), axis=-1).astype(v.dtype)
        p_sel = p[..., :topk * BS].reshape(B, H, QC, topk, BS)
        p_own = p[..., topk * BS:]
        return (jnp.einsum('bhqjk,bhqjkd->bhqd', p_sel, v_sel)
                + jnp.einsum('bhqk,bhkd->bhqd', p_own, v_own))

    o = lax.map(one_chunk, (qc, jnp.arange(nc)))
    return o.transpose(1, 0, 3, 2, 4).reshape(B, S, MOBA_WIDTH)


def setup_inputs(seed: int = 0) -> dict:
    key = jax.random.key(seed)
    ks = jax.random.split(key, 12)
    L, D = DEPTH, D_MODEL
    x = jax.random.normal(ks[0], (BATCH, SEQ, D), jnp.float32)
    w_in = jax.random.normal(ks[1], (L, D, 3 * MIX_WIDTH), jnp.float32) * D ** -0.5
    col = jnp.arange(3 * MIX_WIDTH)
    is_v = ((col >= 2 * DIFF_WIDTH) & (col < 3 * DIFF_WIDTH)) | (col >= 3 * DIFF_WIDTH + 2 * MOBA_WIDTH)
    w_in = w_in * jnp.where(is_v, BETA, 1.0).astype(jnp.float32)
    diff_lambda = jax.random.normal(ks[2], (L, 4, HEAD_DIM), jnp.float32) * 0.1
    diff_subln_g = 1.0 + 0.02 * jax.random.normal(ks[3], (L, 2 * HEAD_DIM), jnp.float32)
    w_o = jax.random.normal(ks[4], (L, MIX_WIDTH, D), jnp.float32) * MIX_WIDTH ** -0.5 * BETA
    ln1_g = 1.0 + 0.02 * jax.random.normal(ks[5], (L, D), jnp.float32)
    ln1_b = 0.02 * jax.random.normal(ks[6], (L, D), jnp.float32)
    w_ffn_in = jax.random.normal(ks[7], (L, D, 2 * D_FF), jnp.float32) * D ** -0.5 * BETA
    w_ffn_out = jax.random.normal(ks[8], (L, D_FF, D), jnp.float32) * D_FF ** -0.5 * BETA
    ln2_g = 1.0 + 0.02 * jax.random.normal(ks[9], (L, D), jnp.float32)
    ln2_b = 0.02 * jax.random.normal(ks[10], (L, D), jnp.float32)
    return {"x": x, "w_in": w_in, "diff_lambda": diff_lambda, "diff_subln_g": diff_subln_g,
            "w_o": w_o, "ln1_g": ln1_g, "ln1_b": ln1_b, "w_ffn_in": w_ffn_in,
            "w_ffn_out": w_ffn_out, "ln2_g": ln2_g, "ln2_b": ln2_b}


def reference(x, w_in, diff_lambda, diff_subln_g, w_o, ln1_g, ln1_b, w_ffn_in, w_ffn_out, ln2_g, ln2_b):
    S = x.shape[1]
    cos, sin = rope_tables(S)
    splits = [DIFF_WIDTH, 2 * DIFF_WIDTH, 3 * DIFF_WIDTH,
              3 * DIFF_WIDTH + MOBA_WIDTH, 3 * DIFF_WIDTH + 2 * MOBA_WIDTH]
    for l in range(DEPTH):
        lambda_init = lambda_init_for(l)
        proj = jnp.einsum('bsd,de->bse', x, w_in[l])
        dq, dk, dv, mq, mk, mv = jnp.split(proj, splits, axis=-1)
        a_out = diff_attention(dq, dk, dv, diff_lambda[l], diff_subln_g[l], lambda_init, cos, sin)
        b_out = moba_attention(mq, mk, mv, cos, sin)
        mix = jnp.einsum('bse,ed->bsd', jnp.concatenate([a_out, b_out], axis=-1), w_o[l])
        x = layer_norm(ALPHA * x + mix, ln1_g[l], ln1_b[l])
        gu = jnp.einsum('bsd,df->bsf', x, w_ffn_in[l])
        g, u = jnp.split(gu, 2, axis=-1)
        f = jnp.einsum('bsf,fd->bsd', jax.nn.silu(g) * u, w_ffn_out[l])
        x = layer_norm(ALPHA * x + f, ln2_g[l], ln2_b[l])
    return x
```

```python
import math
import os
import numpy as np
import concourse.bass as bass
import concourse.mybir as mybir
from concourse.bass_utils import run_bass_kernel_spmd

F32 = mybir.dt.float32
BF16 = mybir.dt.bfloat16
ALU = mybir.AluOpType
AF = mybir.ActivationFunctionType
AX = mybir.AxisListType

D = 1024
S = 4096
NB = 4
DFF = 2816
ALPHA = 2.0 ** 0.25
EPS = 1e-5
LAMBDA_INIT = 0.8 - 0.6 * math.exp(0.0)
NEG = -30000.0
SCALE = 0.125
ENGS = ("pe", "act", "dve", "pool", "sp")


class Op:
    __slots__ = ("eng", "fn", "deps", "sig", "sigval", "dma_sem", "dma_val", "pos")

    def __init__(self, eng, fn):
        self.eng = eng
        self.fn = fn
        self.deps = {}
        self.sig = False
        self.sigval = 0
        self.dma_sem = None
        self.dma_val = 0
        self.pos = 0


class _Rec:
    def __init__(self):
        self.call = None

    def __getattr__(self, name):
        def f(*a, **k):
            self.call = (name, a, k)
            return self
        return f


class Prog:
    def __init__(self, nc):
        self.nc = nc
        self.ops = {e: [] for e in ENGS}
        self.last_w = {}
        self.readers = {}
        self.dma_cnt = {}
        self.barrier_deps = []
        self.out_dmas = []

    def _add_dep(self, o, d):
        if d is None or d is o:
            return
        if d.dma_sem is not None:
            k = ("dma", id(d.dma_sem))
            cur = o.deps.get(k)
            if cur is None or cur.dma_val < d.dma_val:
                o.deps[k] = d
        else:
            cur = o.deps.get(d.eng)
            if cur is None or cur.pos < d.pos:
                o.deps[d.eng] = d

    def op(self, eng, fn, reads=(), writes=(), dma_sem=None):
        rec = _Rec()
        fn(rec)
        o = Op(eng, rec.call)
        o.pos = len(self.ops[eng])
        for d in self.barrier_deps:
            self._add_dep(o, d)
        for k in reads:
            self._add_dep(o, self.last_w.get(k))
        for k in writes:
            self._add_dep(o, self.last_w.get(k))
            for r in self.readers.get(k, {}).values():
                self._add_dep(o, r)
        if dma_sem is not None:
            o.dma_sem = dma_sem
            self.dma_cnt[id(dma_sem)] = self.dma_cnt.get(id(dma_sem), 0) + 16
            o.dma_val = self.dma_cnt[id(dma_sem)]
        for k in reads:
            self.readers.setdefault(k, {})[(eng, o.dma_sem is not None and id(o.dma_sem))] = o
        for k in writes:
            self.last_w[k] = o
            self.readers[k] = {}
        self.ops[eng].append(o)
        return o

    def barrier(self):
        deps = []
        for e in ENGS:
            for o in reversed(self.ops[e]):
                if o.dma_sem is None:
                    deps.append(o)
                    break
        seen = set()
        for e in ENGS:
            for o in reversed(self.ops[e]):
                if o.dma_sem is not None and id(o.dma_sem) not in seen:
                    seen.add(id(o.dma_sem))
                    deps.append(o)
        self.barrier_deps = deps

    def finalize(self, sems):
        for e in ENGS:
            for o in self.ops[e]:
                for d in o.deps.values():
                    if d.dma_sem is None:
                        if d.eng == "pe" and o.eng == "pe" and o.dma_sem is None:
                            continue
                        d.sig = True
        for e in ENGS:
            c = 0
            for o in self.ops[e]:
                if o.dma_sem is None and o.sig:
                    c += 1
                    o.sigval = c
        nc = self.nc
        prog = self

        def run(eng_name, eng):
            waited = {}
            for o in prog.ops[eng_name]:
                for d in o.deps.values():
                    if d.dma_sem is not None:
                        sem, val = d.dma_sem, d.dma_val
                    else:
                        if d.eng == "pe" and o.eng == "pe" and o.dma_sem is None:
                            continue
                        sem, val = sems[d.eng], d.sigval
                    if waited.get(id(sem), 0) >= val:
                        continue
                    waited[id(sem)] = val
                    eng.wait_ge(sem, val)
                name, a, k = o.fn
                ins = getattr(eng, name)(*a, **k)
                if o.dma_sem is not None:
                    ins.then_inc(o.dma_sem, 16)
                elif o.sig:
                    ins.then_inc(sems[eng_name], 1)
            if eng_name == "sp":
                for o in prog.out_dmas:
                    if waited.get(id(o.dma_sem), 0) < o.dma_val:
                        waited[id(o.dma_sem)] = o.dma_val
                        eng.wait_ge(o.dma_sem, o.dma_val)

        with nc.Block() as block:
            @block.tensor
            def _(t):
                run("pe", t)

            @block.scalar
            def _(s):
                run("act", s)

            @block.vector
            def _(v):
                run("dve", v)

            @block.gpsimd
            def _(g):
                run("pool", g)

            @block.sync
            def _(sp):
                run("sp", sp)


def ffn_groups():
    sizes = [4, 4, 4, 4, 3, 3]
    out, c = [], 0
    for s in sizes:
        out.append((c, s))
        c += s
    return out


def build_nc(stage="full"):
    nc = bass.Bass("TRN2", target_bir_lowering=False)
    P = Prog(nc)

    def din(name, shape, dt=F32):
        return nc.dram_tensor(name, list(shape), dt, kind="ExternalInput").ap()

    xT_d = din("xT", [D, S])
    xown_d = din("xown", [2048, D])
    wu_d = din("wu", [8, D, 384])
    wo_d = din("wo", [D, D])
    wg_d = din("wg", [D, DFF])
    wup_d = din("wup", [D, DFF])
    wout_d = din("wout", [DFF, D])
    lnp_d = din("lnp", [4, 128, D])
    lam_d = din("lam", [128, 256])
    subg_d = din("subg", [128, 128])
    rope_d = din("rope", [128, 32 * 32])
    cmask_d = din("cmask", [128, 512])
    ident_d = din("ident", [128, 128])
    omask_d = din("omask", [128, 1])
    bmask_d = din("bmask", [128, 512])
    osel_d = din("osel", [128, 512])
    onehot_d = din("onehot", [16, S])
    out_d = nc.dram_tensor("out", [2048, D], F32, kind="ExternalOutput").ap()
    dbg_d = None
    if stage != "full":
        dbg_d = nc.dram_tensor("dbg", [2048, D], F32, kind="ExternalOutput").ap()

    cur = [16512]

    def sb(name, shape, dt, at=None):
        nbytes = int(np.prod(shape[1:])) * (4 if dt == F32 else 2)
        if at is None:
            off = cur[0]
            cur[0] = off + ((nbytes + 31) // 32) * 32
        else:
            off = at
        assert off % 32 == 0 and off + nbytes <= 229344, (name, off, nbytes)
        return nc.alloc_sbuf_tensor_at(name, list(shape), dt, offset=off)

    ident = sb("ident", [128, 128], BF16)
    cmask = sb("cmaskb", [128, 2, 256], BF16)
    scal = sb("scal", [128, 64], F32)
    subg = sb("subgs", [128, 128], F32)
    rope = sb("ropes", [128, 32, 32], F32)
    bmask = sb("bmasks", [128, 512], F32)
    osel = sb("osels", [128, 512], F32)
    lnp = sb("lnps", [128, 2, D], F32)
    omask = sb("omasks", [128, 8], F32)
    m05 = sb("m05", [128, 8], F32)
    R1 = cur[0]
    xTb = sb("xTb", [128, 8, S], BF16)
    R2 = cur[0]
    wus = [sb("wu0", [128, 8, 384], BF16), sb("wu1", [128, 8, 384], BF16)]
    KT = sb("KT", [128, S], BF16)
    QT = sb("QT", [128, 2048], BF16)
    KA = sb("KA", [128, S], BF16)
    KB = sb("KB", [128, S], BF16)
    QA = sb("QA", [128, 2048], BF16)
    QB = sb("QB", [128, 2048], BF16)
    R3 = cur[0]
    vaug = sb("vaug", [128, 32, 130], BF16)
    pvs = sb("pvs", [128, 2, 258], F32)
    stg = [sb("stg0", [128, 384], F32), sb("stg1", [128, 384], F32)]
    tmK = [sb("tmK0", [128, 4, 128], BF16), sb("tmK1", [128, 4, 128], BF16)]
    tmQ = [sb("tmQ0", [128, 4, 128], BF16), sb("tmQ1", [128, 4, 128], BF16)]
    rsA = sb("rsA", [128, 64], F32)
    rsB = sb("rsB", [128, 64], F32)
    PT = [sb("PT%d" % i, [128, 1024], BF16) for i in range(3)]
    fo = sb("fo", [128, 2, 128], F32)
    ft = sb("ft", [128, 2, 128], F32)
    fs = sb("fs", [128, 32], F32)
    gm = sb("gm", [128, 512], F32)
    gsel = sb("gsel", [128, 512], F32)
    gmx = sb("gmx", [128, 32, 8], F32)
    gthr = sb("gthr", [128, 32], F32)
    biaspad = sb("biaspad", [128, 16, 128], BF16)
    ksum = sb("ksum", [128, 16], F32)
    kmT = sb("kmT", [128, 16], BF16)
    R3end = cur[0]
    cat = sb("cat", [128, 16, D], BF16)
    R4end = cur[0]
    acc = sb("acc", [128, 16, D], F32, at=R1)
    wo_s = sb("wo_s", [128, 8, D], BF16, at=R2)
    x1T = sb("x1T", [128, 8, 2048], BF16, at=R2 + 16384)
    o3 = [R3]

    def sb3(name, shape, dt):
        nbytes = int(np.prod(shape[1:])) * (4 if dt == F32 else 2)
        off = o3[0]
        o3[0] = off + ((nbytes + 31) // 32) * 32
        assert o3[0] <= R3end, (name, o3[0], R3end)
        return nc.alloc_sbuf_tensor_at(name, list(shape), dt, offset=off)

    catT = [sb3("catT0", [128, 8, 128], BF16), sb3("catT1", [128, 8, 128], BF16)]
    yb2 = [sb3("yb", [128, D], F32), sb3("ybb", [128, D], F32)]
    x1bf2 = [sb3("x1bf", [128, D], BF16), sb3("x1bfb", [128, D], BF16)]
    lnst2 = [sb3("lnst", [128, 32], F32), sb3("lnstb", [128, 32], F32)]
    hT = [sb3("hT0", [128, 4, 512], BF16), sb3("hT1", [128, 4, 512], BF16)]
    sg = [sb3("sg0", [128, 512], F32), sb3("sg1", [128, 512], F32)]
    otmp = [sb3("otmp0", [128, 512], F32), sb3("otmp1", [128, 512], F32)]
    Gs = [sb("G0", [128, 8, 512], BF16, at=R3end), sb("G1", [128, 8, 512], BF16, at=R3end + 16384)]
    Us = [sb("U0", [128, 8, 512], BF16, at=R3end + 8192), sb("U1", [128, 8, 512], BF16, at=R3end + 24576)]
    Ws = [sb("W0", [128, 4, D], BF16, at=R2), sb("W1", [128, 4, D], BF16, at=R2 + 8192)]
    ot = [sb("ot0", [128, D], F32, at=R4end), sb("ot1", [128, D], F32, at=R4end + 4096)]
    xt = [sb("xt0", [128, D], F32, at=R4end), sb("xt1", [128, D], F32, at=R4end + 4096)]
    vaug1 = sb("vaug1", [128, 32, 130], BF16, at=R4end)
    vaugs = [vaug, vaug1]

    PT2 = [nc.alloc_psum_tensor("pq%d" % i, [128, 1024], F32) for i in range(4)]
    pstride = PT2[0][:].ap[0][0]

    sems = {e: nc.alloc_semaphore("c_" + e) for e in ENGS}
    dsem = {}

    def DS(name):
        if name not in dsem:
            dsem[name] = nc.alloc_semaphore("d_" + name)
        return dsem[name]

    def AP(t, off, dims):
        ps = t[:].ap[0][0]
        return bass.AP(t, off, [[ps, dims[0]]] + [list(d) for d in dims[1:]])

    def APp(t, p0, off, dims):
        base = t[p0:p0 + dims[0]]
        return bass.AP(base.tensor, base.offset + off, [[base.ap[0][0], dims[0]]] + [list(d) for d in dims[1:]])

    def dma(eng, out, in_, sem, reads=(), writes=()):
        return P.op(eng, lambda e: e.dma_start(out=out, in_=in_), reads=reads, writes=writes, dma_sem=DS(sem))

    dma("sp", rope[:].rearrange("p a b -> p (a b)"), rope_d, "c0", writes=["rope"])
    dma("sp", bmask[:], bmask_d, "c1", writes=["bmask"])
    dma("sp", osel[:], osel_d, "c2", writes=["osel"])
    dma("sp", omask[:, 0:1], omask_d, "c3", writes=["omask"])
    dma("sp", subg[:], subg_d, "c4", writes=["subg"])
    dma("sp", gm[:, 0:256], lam_d, "c5", writes=["gm"])
    dma("sp", lnp[:, 0, :], lnp_d[0], "c6", writes=["lnp0"])
    dma("sp", lnp[:, 1, :], lnp_d[1], "c7", writes=["lnp1"])
    xT_v = xT_d.rearrange("(kc p) t -> p kc t", p=128)
    wu_v = wu_d.rearrange("u (kc p) c -> u p kc c", p=128)

    def load_wu(pos):
        dma("pool", wus[pos % 2][:], wu_v[[0, 4, 1, 5, 2, 6, 3, 7][pos]], "wu%d" % (pos % 2), writes=[("wu", pos % 2)])

    def load_xT(blk):
        dma("pool", xTb[:, :, blk * 512:(blk + 1) * 512], xT_v[:, :, blk * 512:(blk + 1) * 512],
            "xT%d" % blk, writes=[("xT", blk)])

    load_wu(0)
    load_xT(0)
    dma("pool", ident[:], ident_d, "c8", writes=["ident"])
    dma("pool", cmask[:].rearrange("p a b -> p (a b)"), cmask_d, "c9", writes=["cmask"])
    load_xT(1)
    load_wu(1)
    for blk in range(2, 8):
        load_xT(blk)
    for nm, t in (("KA", KA), ("KB", KB), ("QA", QA), ("QB", QB)):
        keys = [(nm, i) for i in range(8)] + [(nm + "a", i) for i in range(2)]
        P.op("pool", lambda e, t=t: e.memset(t[:], 0.0), writes=keys)
    P.op("pool", lambda e: e.memset(biaspad[:], 0.0), writes=["biaspad"])
    for vs_ in range(2):
        P.op("pool", lambda e, vs_=vs_: e.memset(vaugs[vs_][:], 1.0), writes=[("V", vs_, t_) for t_ in range(32)])
    P.op("pool", lambda e: e.memset(m05[:], -0.5), writes=["m05"])
    for hh in range(4):
        dma("pool", KA[64:80, hh * 1024:(hh + 1) * 1024], onehot_d[:, hh * 1024:(hh + 1) * 1024], "c10", writes=[("KA", i) for i in range(8)])
        dma("pool", KB[32:48, hh * 1024:(hh + 1) * 1024], onehot_d[:, hh * 1024:(hh + 1) * 1024], "c11", writes=[("KB", i) for i in range(8)])

    lamv = gm[:, 0:256].rearrange("p (a b) -> p a b", a=4)
    P.op("dve", lambda e: e.tensor_tensor(gsel[:, 0:128].rearrange("p (a b) -> p a b", a=2),
                                          lamv[:, 0:4:2, :], lamv[:, 1:4:2, :], ALU.mult),
         reads=["gm"], writes=["gsel"])
    P.op("dve", lambda e: e.tensor_reduce(scal[:, 0:2], gsel[:, 0:128].rearrange("p (a b) -> p a b", a=2), AX.X, ALU.add),
         reads=["gsel"], writes=["scal01"])
    P.op("act", lambda e: e.activation(scal[:, 2:4], scal[:, 0:2], AF.Exp), reads=["scal01"], writes=["scal23"])
    P.op("dve", lambda e: e.tensor_tensor(scal[:, 4:5], scal[:, 3:4], scal[:, 2:3], ALU.subtract),
         reads=["scal23"], writes=["scal4"])
    P.op("dve", lambda e: e.tensor_scalar(scal[:, 5:6], scal[:, 4:5], -LAMBDA_INIT, None, ALU.add),
         reads=["scal4"], writes=["neglam"])
    neglam = scal[:, 5:6]
    P.op("dve", lambda e: e.tensor_scalar(subg[:], subg[:], 1.0 - LAMBDA_INIT, None, ALU.mult),
         reads=["subg"], writes=["subg"])

    T0 = PT2[0]
    PJ = T0[:, 0:512]
    TP = T0[:, 512:1024]
    TPb = T0[:].bitcast(BF16)[:, 1024:2048]
    STt = [PT2[1], PT2[2]]
    PVts = [PT2[3], PT2[3]]
    PVbk = [(6, 7), (6, 7)]
    vaug5s = [v_[:].rearrange("p t (h c) -> p t h c", h=2) for v_ in vaugs]
    ORDER = [0, 4, 1, 5, 2, 6, 3, 7]

    KP = int(os.environ.get("K_PSTEP", "9"))

    def copy_op(eng, dst, src, reads, writes):
        if eng == "act":
            P.op("act", lambda e: e.activation(dst, src, AF.Copy), reads=reads, writes=writes)
        else:
            P.op("dve", lambda e: e.tensor_copy(dst, src), reads=reads, writes=writes)

    def project_unit(pos):
        u = ORDER[pos]
        moba = u >= 4
        w = wus[pos % 2]
        wk = ("wu", pos % 2)
        vs_ = pos % 2
        vaug_s, vaug5_s = vaugs[vs_], vaug5s[vs_]
        cpe = "act" if pos == 0 else "dve"
        jobi = [0]
        if pos == 0:
            PJB = [(PT2[0], 0, ("BK", 0)), (PT2[1], 0, ("BK", 2)), (PT2[1], 512, ("BK", 3)),
                   (PT2[2], 0, ("BK", 4)), (PT2[2], 512, ("BK", 5))]
        else:
            PJB = [(PT2[0], 0, ("BK", 0)), (PT2[0], 512, ("BK", 1))]

        def job(tt, c0, ncols):
            h = PJB[jobi[0] % len(PJB)]
            jobi[0] += 1
            pj = h[0][:, h[1]:h[1] + ncols]
            for kc in range(8):
                P.op("pe", lambda e, kc=kc, pj=pj: e.matmul(pj, xTb[:, kc, tt * 128:(tt + 1) * 128],
                                                            w[:, kc, c0:c0 + ncols], start=(kc == 0), stop=(kc == 7)),
                     reads=[("xT", tt // 4), wk], writes=[h[2]])
            si = jobi[0] % 2
            P.op("dve", lambda e: e.tensor_copy(stg[si][:, 0:ncols], pj), reads=[h[2]], writes=[("stg", si)])
            return (stg[si], 0, ("stg", si))

        def rope_job(h, tt, nmaps, dsts):
            base = h[1]
            PJT = h[0]
            hk = h[2]
            xs = AP(PJT, base, [128, [64, nmaps], [1, 16]])
            xsw = AP(PJT, base + 8, [128, [64, nmaps], [-8, 2], [1, 8]])
            cc = AP(rope, tt * 32, [128, [0, nmaps], [1, 16]])
            ss = AP(rope, tt * 32 + 16, [128, [0, nmaps], [8, 2], [1, 8]])
            a_o = AP(rsA, 0, [128, [16, nmaps], [1, 16]])
            b_o = AP(rsB, 0, [128, [16, nmaps], [8, 2], [1, 8]])
            P.op("dve", lambda e: e.tensor_tensor(a_o, xs, cc, ALU.mult), reads=[hk, "rope"], writes=["rsA"])
            P.op("dve", lambda e: e.tensor_tensor(b_o, xsw, ss, ALU.mult), reads=[hk, "rope"], writes=["rsB"])
            for i, (dt_, key, doff) in enumerate(dsts):
                a_i = AP(rsA, i * 32, [128, [16, 2], [1, 16]])
                b_i = AP(rsB, i * 32, [128, [16, 2], [1, 16]])
                d_r = AP(dt_, doff, [128, [64, 2], [1, 16]])
                d_c = AP(dt_, doff + 16, [128, [64, 2], [1, 48]])
                src = AP(PJT, base + i * 128 + 16, [128, [64, 2], [1, 48]])
                P.op("dve", lambda e, d_r=d_r, a_i=a_i, b_i=b_i: e.tensor_tensor(d_r, a_i, b_i, ALU.add),
                     reads=["rsA", "rsB"], writes=[key])
                copy_op(cpe, d_c, src, [hk], [key])

        def vcopy(h, c0, tt):
            src = h[0][:, h[1] + c0:h[1] + c0 + 128]
            if not moba:
                dst = vaug_s[:, tt, 0:128]
            else:
                dst = vaug5_s[:, tt, :, 0:64]
                src = src.rearrange("p (h c) -> p h c", h=2)
            copy_op(cpe, dst, src, [h[2]], [("V", vs_, tt)])

        for blk in range(8):
            own = blk < 4
            sl = blk % 2
            for t4 in range(4):
                tt = blk * 4 + t4
                kkey = ("tmK", sl, t4)
                qkey = ("tmQ", sl, t4)
                if own:
                    h = job(tt, 0, 384)
                    rope_job(h, tt, 4, [(tmQ[sl], qkey, t4 * 128), (tmK[sl], kkey, t4 * 128)])
                    vcopy(h, 256, tt)
                else:
                    h = job(tt, 128, 256)
                    rope_job(h, tt, 2, [(tmK[sl], kkey, t4 * 128)])
                    vcopy(h, 128, tt)
                yield
            for t4 in range(4):
                P.op("pe", lambda e, t4=t4: e.transpose(TPb[:, t4 * 128:(t4 + 1) * 128], tmK[sl][:, t4, :], ident[:]),
                     reads=[("tmK", sl, t4), "ident"], writes=[("BK", 1)])
            if own:
                for t4 in range(4):
                    P.op("pe", lambda e, t4=t4: e.transpose(TPb[:, 512 + t4 * 128:512 + (t4 + 1) * 128], tmQ[sl][:, t4, :], ident[:]),
                         reads=[("tmQ", sl, t4), "ident"], writes=[("BK", 1)])
            c0, c1 = blk * 512, (blk + 1) * 512
            if not moba:
                P.op("dve", lambda e: e.tensor_copy(KT[:, c0:c1], TPb[:, 0:512]), reads=[("BK", 1)], writes=[("KT", blk)])
                if own:
                    P.op("dve", lambda e: e.tensor_copy(QT[:, c0:c1], TPb[:, 512:1024]), reads=[("BK", 1)], writes=[("QT", blk)])
            else:
                P.op("dve", lambda e: e.tensor_copy(KA[0:64, c0:c1], TPb[0:64, 0:512]), reads=[("BK", 1)], writes=[("KA", blk)])
                P.op("dve", lambda e: e.tensor_copy(KB[64:128, c0:c1], TPb[64:128, 0:512]), reads=[("BK", 1)], writes=[("KB", blk)])
                if own:
                    P.op("dve", lambda e: e.tensor_copy(QA[0:64, c0:c1], TPb[0:64, 512:1024]), reads=[("BK", 1)], writes=[("QA", blk)])
                    P.op("dve", lambda e: e.tensor_copy(QB[64:128, c0:c1], TPb[64:128, 512:1024]), reads=[("BK", 1)], writes=[("QB", blk)])
            yield

    def moba_gate(u):
        kall = [("KA", i) for i in range(8)]
        kbll = [("KB", i) for i in range(8)]
        P.op("dve", lambda e: e.tensor_reduce(ksum[0:64, :], KA[0:64, :].rearrange("p (n k) -> p n k", n=16), AX.X, ALU.add),
             reads=kall, writes=["ksumA"])
        P.op("dve", lambda e: e.tensor_reduce(ksum[64:128, :], KB[64:128, :].rearrange("p (n k) -> p n k", n=16), AX.X, ALU.add),
             reads=kbll, writes=["ksumB"])
        P.op("dve", lambda e: e.tensor_scalar(kmT[:], ksum[:], 1.0 / 256.0, None, ALU.mult),
             reads=["ksumA", "ksumB"], writes=["kmT"])
        for _ in range(6):
            yield
        for qt in range(16):
            P.op("pe", lambda e, qt=qt: e.matmul(PJ[:, qt * 16:qt * 16 + 16], QA[0:64, qt * 128:(qt + 1) * 128], kmT[0:64, :],
                                                 start=True, stop=True),
                 reads=[("QA", qt // 4), "kmT"], writes=[("BK", 0)])
            P.op("pe", lambda e, qt=qt: e.matmul(TP[:, qt * 16:qt * 16 + 16], QB[64:128, qt * 128:(qt + 1) * 128], kmT[64:128, :],
                                                 start=True, stop=True),
                 reads=[("QB", qt // 4), "kmT"], writes=[("BK", 1)])
        P.op("dve", lambda e: e.tensor_tensor(gm[:, 0:256], PJ[:, 0:256], bmask[:, 0:256], ALU.add),
             reads=[("BK", 0), "bmask"], writes=["gm"])
        P.op("dve", lambda e: e.tensor_tensor(gm[:, 256:512], TP[:, 0:256], bmask[:, 256:512], ALU.add),
             reads=[("BK", 1), "bmask"], writes=["gm"])
        for g in range(32):
            P.op("dve", lambda e, g=g: e.max(gmx[:, g, :], gm[:, g * 16:(g + 1) * 16]), reads=["gm"], writes=["gmx"])
        P.op("dve", lambda e: e.tensor_scalar(gthr[:], gmx[:, :, 2], NEG / 2, None, ALU.max), reads=["gmx"], writes=["gthr"])
        gm3 = gm[:].rearrange("p (g n) -> p g n", g=32)
        gs3 = gsel[:].rearrange("p (g n) -> p g n", g=32)
        thr_b = AP(gthr, 0, [128, [1, 32], [0, 16]])
        P.op("dve", lambda e: e.tensor_tensor(gs3, gm3, thr_b, ALU.is_ge), reads=["gm", "gthr"], writes=["gsel"])
        P.op("dve", lambda e: e.tensor_tensor(gsel[:], gsel[:], osel[:], ALU.add), reads=["gsel", "osel"], writes=["gsel"])
        gs4 = gsel[:].rearrange("p (h q n) -> p h q n", q=16, h=2)
        P.op("dve", lambda e: e.tensor_scalar(biaspad[:, :, 64:80], gs4[:, 0, :, :], -1.0, -NEG, ALU.add, ALU.mult),
             reads=["gsel"], writes=["biaspad"])
        P.op("dve", lambda e: e.tensor_scalar(biaspad[:, :, 32:48], gs4[:, 1, :, :], -1.0, -NEG, ALU.add, ALU.mult),
             reads=["gsel"], writes=["biaspad"])
        for _ in range(6):
            yield
        for half in range(2):
            for q8 in range(8):
                qt = half * 8 + q8
                P.op("pe", lambda e, qt=qt, q8=q8: e.transpose(TPb[:, q8 * 128:(q8 + 1) * 128], biaspad[:, qt, :], ident[:]),
                     reads=["biaspad", "ident"], writes=[("BK", 1)])
            c0, c1 = half * 1024, (half + 1) * 1024
            P.op("dve", lambda e, c0=c0, c1=c1: e.tensor_copy(QA[64:80, c0:c1], TPb[64:80, :]), reads=[("BK", 1)], writes=[("QAa", half)])
            P.op("dve", lambda e, c0=c0, c1=c1: e.tensor_copy(QB[32:48, c0:c1], TPb[32:48, :]), reads=[("BK", 1)], writes=[("QBa", half)])

    grp_ctr = [0]

    def attention_unit(pos, filler, n_fill_=40):
        u = ORDER[pos]
        moba = u >= 4
        vs_ = pos % 2
        vaug_s, vaug5_s = vaugs[vs_], vaug5s[vs_]
        n_fill = n_fill_ if filler is not None else 0
        fill_done = [0]
        items = []
        nch = int(os.environ.get("K_CH", "8"))
        for j in range(nch):
            gl = [("own", i) for i in range(j + 1)] + [("oth", i) for i in range(j + 1)]
            for gi_, (kind, i) in enumerate(gl):
                items.append(dict(j=j, kind=kind, i=i, first=(gi_ == 0), last=(gi_ == len(gl) - 1)))

        def qk_exp(it):
            j, kind, i = it["j"], it["kind"], it["i"]
            gi = grp_ctr[0]
            grp_ctr[0] += 1
            it["gi"] = gi
            qc0 = j * 256
            stt = STt[gi % 2]
            sb_ = (stt[:, 0:512], stt[:, 512:1024])
            skeys = [("BK", 2 + 2 * (gi % 2)), ("BK", 3 + 2 * (gi % 2))]
            pt = PT[gi % 3]
            pkey = ("PT", gi % 3)
            blk = i if kind == "own" else 8 + i
            diag = (kind == "own" and i == j)
            it["blk"], it["diag"] = blk, diag
            for kt in range(2):
                kc0 = blk * 256 + kt * 128
                for m in range(2):
                    o_ap = sb_[m][:, kt * 256:(kt + 1) * 256]
                    if not moba:
                        lh = KT[m * 64:(m + 1) * 64, kc0:kc0 + 128]
                        rh = QT[m * 64:(m + 1) * 64, qc0:qc0 + 256]
                        rk = [("KT", blk // 2), ("QT", j // 2)]
                    else:
                        kt_, qt_ = (KA, QA) if m == 0 else (KB, QB)
                        nm = "A" if m == 0 else "B"
                        lh = kt_[:, kc0:kc0 + 128]
                        rh = qt_[:, qc0:qc0 + 256]
                        rk = [("K" + nm, blk // 2), ("Q" + nm, j // 2), ("Q" + nm + "a", j // 4)]
                    P.op("pe", lambda e, o_ap=o_ap, lh=lh, rh=rh: e.matmul(o_ap, lh, rh, start=True, stop=not diag),
                         reads=rk, writes=skeys)
                    if diag:
                        P.op("pe", lambda e, o_ap=o_ap, kt=kt: e.matmul(o_ap, ident[:], cmask[:, kt, :], start=False, stop=True),
                             reads=["ident", "cmask"], writes=skeys)
            s_in = stt[:].rearrange("p (m c) -> p m c", m=2)
            use_om = (not moba) and kind == "oth" and i == j
            if use_om:
                P.op("act", lambda e, s_in=s_in, pt=pt: e.activation(pt[:].rearrange("p (m c) -> p m c", m=2), s_in, AF.Exp,
                                                                   bias=omask[:, 0:1], scale=SCALE),
                     reads=skeys + ["omask"], writes=[pkey])
            else:
                P.op("act", lambda e, s_in=s_in, pt=pt: e.activation(pt[:].rearrange("p (m c) -> p m c", m=2), s_in, AF.Exp,
                                                                   scale=SCALE),
                     reads=skeys, writes=[pkey])

        def pv(it):
            j, gi, blk, diag = it["j"], it["gi"], it["blk"], it["diag"]
            pt = PT[gi % 3]
            pkey = ("PT", gi % 3)
            PVt = PVts[j % 2]
            PV = [PVt[:, 0:512], PVt[:, 512:1024]]
            bks = PVbk[j % 2]
            for kt in range(2):
                tok_tile = blk * 2 + kt
                for m in range(2):
                    for qt in range(2):
                        if diag and kt == 1 and qt == 0:
                            continue
                        lh = pt[:, m * 512 + kt * 256 + qt * 128: m * 512 + kt * 256 + (qt + 1) * 128]
                        if not moba:
                            rh = vaug_s[:, tok_tile, 0:129]
                            o_ap = PV[m][:, qt * 129:(qt + 1) * 129]
                        else:
                            rh = vaug5_s[:, tok_tile, m, :]
                            o_ap = PV[m][:, qt * 65:(qt + 1) * 65]
                        st = it["first"] and kt == 0 and qt == 0
                        P.op("pe", lambda e, o_ap=o_ap, lh=lh, rh=rh, st=st: e.matmul(o_ap, lh, rh, start=st, stop=False,
                                                                                       skip_group_check=True),
                             reads=[pkey, ("V", vs_, tok_tile)], writes=[("BK", bks[m])])
            if it["last"]:
                finalize_chunk(u, j)

        for idx in range(len(items) + 1):
            if idx < len(items):
                qk_exp(items[idx])
            if idx >= 1:
                pv(items[idx - 1])
            if filler is not None:
                due = min(n_fill, ((idx + 1) * n_fill + len(items) - 1) // len(items))
                while fill_done[0] < due:
                    next(filler, None)
                    fill_done[0] += 1
        if filler is not None:
            for _ in filler:
                pass

    def finalize_chunk(u, j):
        moba = u >= 4
        bks = PVbk[j % 2]
        ncol = 130 if moba else 258
        for m_i in range(2):
            P.op("dve", lambda e, m_i=m_i: e.tensor_copy(pvs[:, m_i, 0:ncol], PVts[0][:, m_i * 512:m_i * 512 + ncol]),
                 reads=[("BK", bks[m_i])], writes=[("pvs", m_i)])
        PVt = pvs
        PV = [pvs[:, 0, :], pvs[:, 1, :]]
        if not moba:
            l0 = AP(pvs, 128, [128, [129, 2]])
            l1 = AP(pvs, 258 + 128, [128, [129, 2]])
            P.op("dve", lambda e: e.reciprocal(fs[:, 0:2], l0), reads=[("pvs", 0)], writes=["fs01"])
            P.op("dve", lambda e: e.reciprocal(fs[:, 2:4], l1), reads=[("pvs", 1)], writes=["fs23"])
            P.op("dve", lambda e: e.tensor_scalar(fs[:, 4:6], fs[:, 2:4], neglam, None, ALU.mult),
                 reads=["fs23", "neglam"], writes=["fs45"])
            for qt in range(2):
                P.op("dve", lambda e, qt=qt: e.tensor_scalar(ft[:, qt, :], PV[1][:, qt * 129:qt * 129 + 128], fs[:, 4 + qt:5 + qt], None, ALU.mult),
                     reads=[("pvs", 1), "fs45"], writes=[("ft", qt)])
                P.op("dve", lambda e, qt=qt: e.scalar_tensor_tensor(fo[:, qt, :], PV[0][:, qt * 129:qt * 129 + 128], fs[:, qt:qt + 1],
                                                                      ft[:, qt, :], ALU.mult, ALU.add),
                     reads=[("pvs", 0), "fs01", ("ft", qt)], writes=[("fo", qt)])
                P.op("dve", lambda e, qt=qt: e.scalar_tensor_tensor(ft[:, qt, :], fo[:, qt, :], 1.0, fo[:, qt, :], ALU.mult, ALU.mult,
                                                                      accum_out=fs[:, 8 + qt:9 + qt]),
                     reads=[("fo", qt)], writes=[("ft", qt), ("fss", qt)])
            P.op("dve", lambda e: e.tensor_scalar(fs[:, 12:14], fs[:, 8:10], 1.0 / 128.0, EPS, ALU.mult, ALU.add),
                 reads=[("fss", 0), ("fss", 1)], writes=["fsv"])
            P.op("pool", lambda e: e.tensor_tensor(fs[:, 16:18], fs[:, 12:14], m05[:, 0:2], ALU.pow),
                 reads=["fsv", "m05"], writes=["fsr"])
            for qt in range(2):
                tile = 2 * j + qt
                P.op("dve", lambda e, qt=qt, tile=tile: e.scalar_tensor_tensor(cat[:, tile, u * 128:(u + 1) * 128], fo[:, qt, :],
                                                                                fs[:, 16 + qt:17 + qt], subg[:], ALU.mult, ALU.mult),
                     reads=[("fo", qt), "fsr", "subg"], writes=[("cat", tile, u)])
        else:
            m_ = u - 4
            for h in range(2):
                lh = AP(pvs, h * 258 + 64, [128, [65, 2]])
                P.op("dve", lambda e, h=h, lh=lh: e.reciprocal(fs[:, 20 + 2 * h:22 + 2 * h], lh), reads=[("pvs", h)], writes=[("fsm", h)])
                for qt in range(2):
                    tile = 2 * j + qt
                    c0 = 512 + (2 * m_ + h) * 64
                    P.op("dve", lambda e, h=h, qt=qt, tile=tile, c0=c0: e.tensor_scalar(cat[:, tile, c0:c0 + 64],
                                                                                         PV[h][:, qt * 65:qt * 65 + 64],
                                                                                         fs[:, 20 + 2 * h + qt:21 + 2 * h + qt], None, ALU.mult),
                         reads=[("pvs", h), ("fsm", h)], writes=[("cat", tile, u)])

    wo_v = wo_d.rearrange("(ec p) d -> p ec d", p=128)
    g0 = project_unit(0)
    for _ in g0:
        pass
    load_wu(2)
    for pos in range(8):
        u = ORDER[pos]
        nxt, nf = None, 40
        if pos + 1 < 8:
            if ORDER[pos + 1] >= 4:
                def chain_(a_, b_):
                    for _ in a_:
                        yield
                    for _ in b_:
                        yield
                nxt, nf = chain_(project_unit(pos + 1), moba_gate(ORDER[pos + 1])), 40 + 13
            else:
                nxt = project_unit(pos + 1)
        if pos == 7:
            ovl = [("wu", 0), ("wu", 1)] + [("KT", b_) for b_ in range(8)]
            for ec in range(8):
                dma("pool", wo_s[:, ec, :], wo_v[:, ec, :], "wo", writes=["wo"] + ovl)
        attention_unit(pos, nxt, nf)
        if pos + 3 < 8:
            load_wu(pos + 3)

    P.barrier()

    if stage == "A":
        dv = dbg_d.rearrange("(t p) d -> p t d", p=128)
        for t in range(16):
            o = dma("pool", dv[:, t, :], cat[:, t, :], "dbg", reads=[("cat", t, uu) for uu in range(8)])
            P.out_dmas.append(o)
        ov = out_d.rearrange("(t p) d -> p t d", p=128)
        P.finalize(sems)
        return nc

    TPc = T0[:].bitcast(BF16)[:, 0:1024]
    TPx = T0[:].bitcast(BF16)[:, 1024:2048]
    MIX = [(PT2[1][:, 0:512], PT2[1][:, 512:1024]), (PT2[2][:, 0:512], PT2[2][:, 512:1024])]
    MIXt = [PT2[1], PT2[2]]
    xown_v = xown_d.rearrange("(t p) d -> p t d", p=128)

    lnB = sb("lnB", [128, 16, 32], F32, at=R4end + 8192)

    def ln_stats(src_ap, skey, s_):
        lnst = lnst2[s_] if isinstance(s_, int) else lnB[:, s_[1], :]
        P.op("dve", lambda e: e.bn_stats(lnst[:, 0:6], src_ap[:, 0:512]), reads=[skey], writes=[("lnst_a", s_)])
        P.op("dve", lambda e: e.bn_stats(lnst[:, 6:12], src_ap[:, 512:1024]), reads=[skey], writes=[("lnst_b", s_)])
        P.op("dve", lambda e: e.bn_aggr(lnst[:, 12:14], lnst[:, 0:12]), reads=[("lnst_a", s_), ("lnst_b", s_)], writes=[("lnmv", s_)])
        P.op("dve", lambda e: e.tensor_scalar(lnst[:, 14:15], lnst[:, 13:14], EPS, None, ALU.add), reads=[("lnmv", s_)], writes=[("lnve", s_)])
        P.op("pool", lambda e: e.tensor_tensor(lnst[:, 16:17], lnst[:, 14:15], m05[:, 0:1], ALU.pow), reads=[("lnve", s_), "m05"], writes=[("lnr", s_)])
        P.op("dve", lambda e: e.scalar_tensor_tensor(lnst[:, 17:18], lnst[:, 12:13], -1.0, lnst[:, 16:17], ALU.mult, ALU.mult),
             reads=[("lnmv", s_), ("lnr", s_)], writes=[("lnnmr", s_)])

    def B1(t):
        sl = t % 2
        dma("sp", xt[sl][:], xown_v[:, t, :], "xt%d" % sl, writes=[("xt", sl)])
        for ec in range(8):
            P.op("pe", lambda e, ec=ec: e.transpose(TPc[:, ec * 128:(ec + 1) * 128], cat[:, t, ec * 128:(ec + 1) * 128], ident[:]),
                 reads=[("cat", t, ec), "ident"], writes=["TPc"])
        P.op("act", lambda e: e.activation(catT[sl][:].rearrange("p a b -> p (a b)"), TPc[:, :], AF.Copy),
             reads=["TPc"], writes=[("catT", sl)])
        mx = MIX[sl]
        for dh in range(2):
            for ec in range(8):
                P.op("pe", lambda e, ec=ec, dh=dh: e.matmul(mx[dh], catT[sl][:, ec, :], wo_s[:, ec, dh * 512:(dh + 1) * 512],
                                                            start=(ec == 0), stop=(ec == 7)),
                     reads=[("catT", sl), "wo"], writes=[("MIX", sl)])
        mix_ap = MIXt[sl][:].rearrange("p (a b) -> p a b", a=2)
        P.op("dve", lambda e: e.scalar_tensor_tensor(acc[:, t, :].rearrange("p (a b) -> p a b", a=2),
                                                     xt[sl][:].rearrange("p (a b) -> p a b", a=2), ALPHA, mix_ap,
                                                     ALU.mult, ALU.add),
             reads=[("xt", sl), ("MIX", sl)], writes=[("acc", t)])
        ln_stats(acc[:, t, :], ("acc", t), ("B", t))

    def B2dve(t):
        s_ = ("B", t)
        P.op("dve", lambda e: e.tensor_scalar(acc[:, t, :], acc[:, t, :], lnB[:, t, 16:17], lnB[:, t, 17:18], ALU.mult, ALU.add),
             reads=[("acc", t), ("lnr", s_), ("lnnmr", s_)], writes=[("acc", t)])
        P.op("dve", lambda e: e.tensor_tensor(acc[:, t, :], acc[:, t, :], lnp[:, 0, :], ALU.mult), reads=[("acc", t), "lnp0"], writes=[("acc", t)])
        P.op("dve", lambda e: e.tensor_tensor(acc[:, t, :], acc[:, t, :], lnp[:, 1, :], ALU.add), reads=[("acc", t), "lnp1"], writes=[("acc", t)])

    def B2rest(t):
        sl = t % 2
        x1bf = x1bf2[sl]
        P.op("act", lambda e: e.activation(x1bf[:], acc[:, t, :], AF.Copy), reads=[("acc", t)], writes=[("x1bf", sl)])
        for kc in range(8):
            P.op("pe", lambda e, kc=kc: e.transpose(TPx[:, kc * 128:(kc + 1) * 128], x1bf[:, kc * 128:(kc + 1) * 128], ident[:]),
                 reads=[("x1bf", sl), "ident"], writes=["TPx"])
        P.op("act", lambda e: e.activation(x1T[:, :, t * 128:(t + 1) * 128], TPx[:, :].rearrange("p (a b) -> p a b", a=8), AF.Copy),
             reads=["TPx"], writes=[("x1T", t // 4)])
        P.op("pool", lambda e: e.tensor_scalar(acc[:, t, :], acc[:, t, :], ALPHA, 1.0, ALU.mult, ALU.mult),
             reads=[("acc", t)], writes=[("acc", t)])

    wg_v = wg_d.rearrange("(kc p) f -> p kc f", p=128)
    wup_v = wup_d.rearrange("(kc p) f -> p kc f", p=128)
    wout_v = wout_d.rearrange("(c p) d -> p c d", p=128)
    groups = ffn_groups()

    def cat_keys(t0, t1):
        return [("cat", t_, e_) for t_ in range(t0, t1) for e_ in range(8)]

    def load_ffn(gi, which="GUW", ag=(), au=(), aw=()):
        c0, n = groups[gi]
        s_ = gi % 2
        if "G" in which:
            dma("pool", Gs[s_][:, :, 0:n * 128], wg_v[:, :, c0 * 128:(c0 + n) * 128], "G%d" % s_, writes=[("G", s_)] + list(ag))
        if "U" in which:
            dma("pool", Us[s_][:, :, 0:n * 128], wup_v[:, :, c0 * 128:(c0 + n) * 128], "U%d" % s_, writes=[("U", s_)] + list(au))
        if "W" in which:
            dma("pool", Ws[s_][:, 0:n, :], wout_v[:, c0:c0 + n, :], "W%d" % s_, writes=[("W", s_)] + list(aw))

    for t in range(16 + 2):
        if t == 10:
            load_ffn(0, "GU", ag=cat_keys(0, 4), au=cat_keys(4, 8))
        if t - 2 >= 0:
            B2dve(t - 2)
        if t < 16:
            B1(t)
        if t - 2 >= 0:
            B2rest(t - 2)

    if stage == "B":
        dv = dbg_d.rearrange("(t p) d -> p t d", p=128)
        for t in range(16):
            o = dma("sp", dv[:, t, :], acc[:, t, :], "dbg", reads=[("acc", t)])
            P.out_dmas.append(o)
        P.finalize(sems)
        return nc

    load_ffn(0, "W", aw=["wo"])
    P.barrier()
    load_ffn(1, "GUW", ag=cat_keys(8, 12), au=cat_keys(12, 16), aw=["wo"])
    dma("sp", lnp[:, 0, :], lnp_d[2], "c6", writes=["lnp0"])
    dma("sp", lnp[:, 1, :], lnp_d[3], "c7", writes=["lnp1"])
    GB = [PT2[0][:, 0:512], PT2[0][:, 512:1024]]
    UB = [PT2[1][:, 0:512], PT2[1][:, 512:1024]]
    OB = [PT2[2][:, 0:512], PT2[2][:, 512:1024]]
    ov = out_d.rearrange("(t p) d -> p t d", p=128)

    def D1a(t):
        s_ = ("B", t)
        lnst = lnB[:, t, :]
        src_ap = acc[:, t, :]
        P.op("dve", lambda e: e.bn_stats(lnst[:, 0:6], src_ap[:, 0:512]), reads=[("acc", t)], writes=[("lnst_a", s_)])
        P.op("dve", lambda e: e.bn_stats(lnst[:, 6:12], src_ap[:, 512:1024]), reads=[("acc", t)], writes=[("lnst_b", s_)])
        P.op("dve", lambda e: e.bn_aggr(lnst[:, 12:14], lnst[:, 0:12]), reads=[("lnst_a", s_), ("lnst_b", s_)], writes=[("lnmv", s_)])
        P.op("dve", lambda e: e.tensor_scalar(lnst[:, 14:15], lnst[:, 13:14], EPS, None, ALU.add), reads=[("lnmv", s_)], writes=[("lnve", s_)])
        P.op("pool", lambda e: e.tensor_tensor(lnst[:, 16:17], lnst[:, 14:15], m05[:, 0:1], ALU.pow), reads=[("lnve", s_), "m05"], writes=[("lnr", s_)])

    def D1b(t):
        s_ = ("B", t)
        lnst = lnB[:, t, :]
        P.op("dve", lambda e: e.scalar_tensor_tensor(lnst[:, 17:18], lnst[:, 12:13], -1.0, lnst[:, 16:17], ALU.mult, ALU.mult),
             reads=[("lnmv", s_), ("lnr", s_)], writes=[("lnnmr", s_)])

    def D2act(t):
        sl = t % 2
        yb, lnst = yb2[sl], lnB[:, t, :]
        P.op("act", lambda e: e.activation(yb[:], acc[:, t, :], AF.Identity, bias=lnst[:, 17:18], scale=lnst[:, 16:17]),
             reads=[("acc", t), ("lnr", ("B", t)), ("lnnmr", ("B", t))], writes=[("yb", sl)])

    def D2dve(t):
        sl = t % 2
        yb = yb2[sl]
        P.op("dve", lambda e: e.tensor_tensor(yb[:], yb[:], lnp[:, 0, :], ALU.mult), reads=[("yb", sl), "lnp0"], writes=[("yb", sl)])
        P.op("pool", lambda e: e.tensor_tensor(ot[sl][:], yb[:], lnp[:, 1, :], ALU.add), reads=[("yb", sl), "lnp1"], writes=[("ot", sl)])
        o = dma("sp", ov[:, t, :], ot[sl][:], "ot%d" % sl, reads=[("ot", sl)])
        P.out_dmas.append(o)

    def D1(t):
        D1a(t)
        D1b(t)

    def D2(t):
        sl = t % 2
        yb, lnst = yb2[sl], lnB[:, t, :]
        P.op("act", lambda e: e.activation(yb[:], acc[:, t, :], AF.Identity, bias=lnst[:, 17:18], scale=lnst[:, 16:17]),
             reads=[("acc", t), ("lnr", ("B", t)), ("lnnmr", ("B", t))], writes=[("yb", sl)])
        P.op("dve", lambda e: e.tensor_tensor(yb[:], yb[:], lnp[:, 0, :], ALU.mult), reads=[("yb", sl), "lnp0"], writes=[("yb", sl)])
        P.op("dve", lambda e: e.tensor_tensor(ot[sl][:], yb[:], lnp[:, 1, :], ALU.add), reads=[("yb", sl), "lnp1"], writes=[("ot", sl)])
        o = dma("sp", ov[:, t, :], ot[sl][:], "ot%d" % sl, reads=[("ot", sl)])
        P.out_dmas.append(o)

    ci = [0]
    oi = [0]
    for gi, (c0, n) in enumerate(groups):
        s_ = gi % 2
        for tb in range(4):
            hs = (gi * 4 + tb) % 2
            for c in range(n):
                b_ = ci[0] % 2
                ci[0] += 1
                for kc in range(8):
                    P.op("pe", lambda e, kc=kc, c=c, b_=b_, s_=s_, tb=tb: e.matmul(GB[b_], Gs[s_][:, kc, c * 128:(c + 1) * 128],
                                                                               x1T[:, kc, tb * 512:(tb + 1) * 512], start=(kc == 0), stop=(kc == 7)),
                         reads=[("G", s_), ("x1T", tb)], writes=[("GB", b_)])
                for kc in range(8):
                    P.op("pe", lambda e, kc=kc, c=c, b_=b_, s_=s_, tb=tb: e.matmul(UB[b_], Us[s_][:, kc, c * 128:(c + 1) * 128],
                                                                               x1T[:, kc, tb * 512:(tb + 1) * 512], start=(kc == 0), stop=(kc == 7)),
                         reads=[("U", s_), ("x1T", tb)], writes=[("UB", b_)])
                P.op("act", lambda e, b_=b_: e.activation(sg[b_][:], GB[b_], AF.Silu), reads=[("GB", b_)], writes=[("sg", b_)])
                P.op("dve", lambda e, b_=b_, c=c, hs=hs: e.tensor_tensor(hT[hs][:, c, :], sg[b_][:], UB[b_], ALU.mult),
                     reads=[("sg", b_), ("UB", b_)], writes=[("hT", hs)])
                if gi == len(groups) - 1 and tb >= 1:
                    pl = [(tb - 1) * 4 + q_ for q_ in range(4)]
                    if c == 0:
                        if tb >= 2:
                            D2dve(pl[0] - 2)
                            D2dve(pl[0] - 1)
                        for t_ in pl:
                            D1a(t_)
                    elif c == 1:
                        for t_ in pl:
                            D1b(t_)
                        D2act(pl[0])
                        D2act(pl[1])
                    elif c == 2:
                        D2dve(pl[0])
                        D2dve(pl[1])
                        D2act(pl[2])
                        D2act(pl[3])
            for t4 in range(4):
                tile = tb * 4 + t4
                for dh in range(2):
                    ob = oi[0] % 2
                    oi[0] += 1
                    for c in range(n):
                        P.op("pe", lambda e, c=c, ob=ob, hs=hs, t4=t4, dh=dh, s_=s_: e.matmul(OB[ob], hT[hs][:, c, t4 * 128:(t4 + 1) * 128],
                                                                                          Ws[s_][:, c, dh * 512:(dh + 1) * 512],
                                                                                          start=(c == 0), stop=(c == n - 1)),
                             reads=[("hT", hs), ("W", s_)], writes=[("OB", ob)])
                    if gi == len(groups) - 1:
                        P.op("act", lambda e, ob=ob: e.activation(otmp[ob][:], OB[ob], AF.Copy), reads=[("OB", ob)], writes=[("otmp", ob)])
                        P.op("pool", lambda e, ob=ob, tile=tile, dh=dh: e.tensor_tensor(acc[:, tile, dh * 512:(dh + 1) * 512],
                                                                                       acc[:, tile, dh * 512:(dh + 1) * 512], otmp[ob][:], ALU.add),
                             reads=[("otmp", ob), ("acc", tile)], writes=[("acc", tile)])
                    else:
                        P.op("dve", lambda e, ob=ob, tile=tile, dh=dh: e.tensor_tensor(acc[:, tile, dh * 512:(dh + 1) * 512],
                                                                                      acc[:, tile, dh * 512:(dh + 1) * 512], OB[ob], ALU.add),
                             reads=[("OB", ob), ("acc", tile)], writes=[("acc", tile)])
            if gi == len(groups) - 1 and tb == 3:
                tl = [tb * 4 + q_ for q_ in range(4)]
                D2dve(tl[0] - 2)
                D2dve(tl[0] - 1)
                for t_ in tl:
                    D1a(t_)
                for t_ in tl:
                    D1b(t_)
                D2act(tl[0])
                D2act(tl[1])
                D2dve(tl[0])
                D2act(tl[2])
                D2dve(tl[1])
                D2act(tl[3])
                D2dve(tl[2])
                D2dve(tl[3])
        if gi + 2 < len(groups):
            load_ffn(gi + 2)

    P.finalize(sems)
    return nc


def _perm_tokens(p):
    own = np.concatenate([np.arange((2 * j + p) * 256, (2 * j + p + 1) * 256) for j in range(8)])
    oth = np.concatenate([np.arange((2 * j + 1 - p) * 256, (2 * j + 2 - p) * 256) for j in range(8)])
    return own, oth


def _const_tables(p):
    own, oth = _perm_tokens(p)
    pos = np.concatenate([own, oth]).astype(np.float64)
    inv = 500000.0 ** (-np.arange(0, 16, 2, dtype=np.float64) / 16.0)
    ang = pos[:, None] * inv[None, :]
    cos, sin = np.cos(ang), np.sin(ang)
    tab = np.concatenate([cos, cos, -sin, sin], axis=1).astype(np.float32)
    rope = tab.reshape(32, 128, 32).transpose(1, 0, 2).reshape(128, 32 * 32).copy()
    k = np.arange(128)[:, None, None]
    kt = np.arange(2)[None, :, None]
    q = np.arange(256)[None, None, :]
    cmask = np.where(q >= kt * 128 + k, 0.0, NEG).astype(np.float32).reshape(128, 512)
    ident = np.eye(128, dtype=np.float32)
    omask = np.full((128, 1), 0.0 if p == 1 else NEG, np.float32)
    bm = np.zeros((2, 16, 16), np.float32)
    os_ = np.zeros((2, 16, 16), np.float32)
    for qt in range(16):
        j = qt // 2
        for n in range(16):
            if n < 8:
                past = n < j
            else:
                i = n - 8
                past = (i < j) if p == 0 else (i <= j)
            bm[:, qt, n] = 0.0 if past else NEG
        os_[:, qt, j] = 1.0
    bmask = np.broadcast_to(bm.reshape(1, 512), (128, 512)).copy()
    osel = np.broadcast_to(os_.reshape(1, 512), (128, 512)).copy()
    onehot = np.zeros((16, S), np.float32)
    for n in range(16):
        onehot[n, n * 256:(n + 1) * 256] = 1.0
    return dict(rope=rope, cmask=cmask, ident=ident, omask=omask, bmask=bmask, osel=osel, onehot=onehot)


def _prep_inputs(x, w_in, diff_lambda, diff_subln_g, w_o, ln1_g, ln1_b, w_ffn_in, w_ffn_out, ln2_g, ln2_b):
    x = np.asarray(x, np.float32)
    w_in = np.asarray(w_in, np.float32)[0]
    dq, dk, dv = w_in[:, 0:512], w_in[:, 512:1024], w_in[:, 1024:1536]
    mq, mk, mv = w_in[:, 1536:2048], w_in[:, 2048:2560], w_in[:, 2560:3072]
    units = []
    for h in range(4):
        sl = slice(h * 128, (h + 1) * 128)
        units.append(np.concatenate([dq[:, sl], dk[:, sl], dv[:, sl]], axis=1))
    for m in range(4):
        sl = slice(m * 128, (m + 1) * 128)
        units.append(np.concatenate([mq[:, sl], mk[:, sl], mv[:, sl]], axis=1))
    wu = np.ascontiguousarray(np.stack(units, 0))
    wffn = np.asarray(w_ffn_in, np.float32)[0]
    shared = dict(
        wu=wu,
        wo=np.ascontiguousarray(np.asarray(w_o, np.float32)[0]),
        wg=np.ascontiguousarray(wffn[:, :DFF]),
        wup=np.ascontiguousarray(wffn[:, DFF:]),
        wout=np.ascontiguousarray(np.asarray(w_ffn_out, np.float32)[0]),
        lnp=np.ascontiguousarray(np.stack([np.broadcast_to(np.asarray(a, np.float32)[0][None, :], (128, D))
                                           for a in (ln1_g, ln1_b, ln2_g, ln2_b)], 0)),
        lam=np.ascontiguousarray(np.broadcast_to(np.asarray(diff_lambda, np.float32)[0].reshape(1, 256), (128, 256))),
        subg=np.ascontiguousarray(np.broadcast_to(np.asarray(diff_subln_g, np.float32)[0][None, :], (128, 128))),
    )
    in_maps = []
    consts = [_const_tables(0), _const_tables(1)]
    for c in range(8):
        b, p = c // 2, c % 2
        own, oth = _perm_tokens(p)
        perm = np.concatenate([own, oth])
        m = dict(shared)
        m.update(consts[p])
        m["xT"] = np.ascontiguousarray(x[b][perm].T)
        m["xown"] = np.ascontiguousarray(x[b][own])
        in_maps.append(m)
    return in_maps


_NC_CACHE = {}


def kernel(x, w_in, diff_lambda, diff_subln_g, w_o, ln1_g, ln1_b, w_ffn_in, w_ffn_out, ln2_g, ln2_b, _stage="full"):
    in_maps = _prep_inputs(x, w_in, diff_lambda, diff_subln_g, w_o, ln1_g, ln1_b, w_ffn_in, w_ffn_out, ln2_g, ln2_b)
    nc = build_nc(_stage)
    res = run_bass_kernel_spmd(nc, in_maps, core_ids=list(range(8)))
    key = "out" if _stage == "full" else "dbg"
    out = np.zeros((NB, S, D), np.float32)
    for c in range(8):
        b, p = c // 2, c % 2
        own, _ = _perm_tokens(p)
        out[b, own] = np.asarray(res.results[c][key], np.float32)
    return out
```

```python
import math
import os
import numpy as np
import concourse.bass as bass
import concourse.mybir as mybir
from concourse.bass_utils import run_bass_kernel_spmd

F32 = mybir.dt.float32
BF16 = mybir.dt.bfloat16
ALU = mybir.AluOpType
AF = mybir.ActivationFunctionType
AX = mybir.AxisListType

D = 1024
S = 4096
NB = 4
DFF = 2816
ALPHA = 2.0 ** 0.25
EPS = 1e-5
LAMBDA_INIT = 0.8 - 0.6 * math.exp(0.0)
NEG = -30000.0
SCALE = 0.125
ENGS = ("pe", "act", "dve", "pool", "sp")


class Op:
    __slots__ = ("eng", "fn", "deps", "sig", "sigval", "dma_sem", "dma_val", "pos")

    def __init__(self, eng, fn):
        self.eng = eng
        self.fn = fn
        self.deps = {}
        self.sig = False
        self.sigval = 0
        self.dma_sem = None
        self.dma_val = 0
        self.pos = 0


class _Rec:
    def __init__(self):
        self.call = None

    def __getattr__(self, name):
        def f(*a, **k):
            self.call = (name, a, k)
            return self
        return f


class Prog:
    def __init__(self, nc):
        self.nc = nc
        self.ops = {e: [] for e in ENGS}
        self.last_w = {}
        self.readers = {}
        self.dma_cnt = {}
        self.barrier_deps = []
        self.out_dmas = []

    def _add_dep(self, o, d):
        if d is None or d is o:
            return
        if d.dma_sem is not None:
            k = ("dma", id(d.dma_sem))
            cur = o.deps.get(k)
            if cur is None or cur.dma_val < d.dma_val:
                o.deps[k] = d
        else:
            cur = o.deps.get(d.eng)
            if cur is None or cur.pos < d.pos:
                o.deps[d.eng] = d

    def op(self, eng, fn, reads=(), writes=(), dma_sem=None):
        rec = _Rec()
        fn(rec)
        o = Op(eng, rec.call)
        o.pos = len(self.ops[eng])
        for d in self.barrier_deps:
            self._add_dep(o, d)
        for k in reads:
            self._add_dep(o, self.last_w.get(k))
        for k in writes:
            self._add_dep(o, self.last_w.get(k))
            for r in self.readers.get(k, {}).values():
                self._add_dep(o, r)
        if dma_sem is not None:
            o.dma_sem = dma_sem
            self.dma_cnt[id(dma_sem)] = self.dma_cnt.get(id(dma_sem), 0) + 16
            o.dma_val = self.dma_cnt[id(dma_sem)]
        for k in reads:
            self.readers.setdefault(k, {})[(eng, o.dma_sem is not None and id(o.dma_sem))] = o
        for k in writes:
            self.last_w[k] = o
            self.readers[k] = {}
        self.ops[eng].append(o)
        return o

    def barrier(self):
        deps = []
        for e in ENGS:
            for o in reversed(self.ops[e]):
                if o.dma_sem is None:
                    deps.append(o)
                    break
        seen = set()
        for e in ENGS:
            for o in reversed(self.ops[e]):
                if o.dma_sem is not None and id(o.dma_sem) not in seen:
                    seen.add(id(o.dma_sem))
                    deps.append(o)
        self.barrier_deps = deps

    def finalize(self, sems):
        for e in ENGS:
            for o in self.ops[e]:
                for d in o.deps.values():
                    if d.dma_sem is None:
                        if d.eng == "pe" and o.eng == "pe" and o.dma_sem is None:
                            continue
                        d.sig = True
        for e in ENGS:
            c = 0
            for o in self.ops[e]:
                if o.dma_sem is None and o.sig:
                    c += 1
                    o.sigval = c
        nc = self.nc
        prog = self

        def run(eng_name, eng):
            waited = {}
            for o in prog.ops[eng_name]:
                for d in o.deps.values():
                    if d.dma_sem is not None:
                        sem, val = d.dma_sem, d.dma_val
                    else:
                        if d.eng == "pe" and o.eng == "pe" and o.dma_sem is None:
                            continue
                        sem, val = sems[d.eng], d.sigval
                    if waited.get(id(sem), 0) >= val:
                        continue
                    waited[id(sem)] = val
                    eng.wait_ge(sem, val)
                name, a, k = o.fn
                ins = getattr(eng, name)(*a, **k)
                if o.dma_sem is not None:
                    ins.then_inc(o.dma_sem, 16)
                elif o.sig:
                    ins.then_inc(sems[eng_name], 1)
            if eng_name == "sp":
                for o in prog.out_dmas:
                    if waited.get(id(o.dma_sem), 0) < o.dma_val:
                        waited[id(o.dma_sem)] = o.dma_val
                        eng.wait_ge(o.dma_sem, o.dma_val)

        with nc.Block() as block:
            @block.tensor
            def _(t):
                run("pe", t)

            @block.scalar
            def _(s):
                run("act", s)

            @block.vector
            def _(v):
                run("dve", v)

            @block.gpsimd
            def _(g):
                run("pool", g)

            @block.sync
            def _(sp):
                run("sp", sp)


def ffn_groups():
    sizes = [4, 4, 4, 4, 3, 3]
    out, c = [], 0
    for s in sizes:
        out.append((c, s))
        c += s
    return out


def build_nc(stage="full"):
    nc = bass.Bass("TRN2", target_bir_lowering=False)
    P = Prog(nc)

    def din(name, shape, dt=F32):
        return nc.dram_tensor(name, list(shape), dt, kind="ExternalInput").ap()

    xT_d = din("xT", [D, S])
    xown_d = din("xown", [2048, D])
    wu_d = din("wu", [8, D, 384])
    wo_d = din("wo", [D, D])
    wg_d = din("wg", [D, DFF])
    wup_d = din("wup", [D, DFF])
    wout_d = din("wout", [DFF, D])
    lnp_d = din("lnp", [4, 128, D])
    lam_d = din("lam", [128, 256])
    subg_d = din("subg", [128, 128])
    rope_d = din("rope", [128, 32 * 32])
    cmask_d = din("cmask", [128, 512])
    ident_d = din("ident", [128, 128])
    omask_d = din("omask", [128, 1])
    bmask_d = din("bmask", [128, 512])
    osel_d = din("osel", [128, 512])
    onehot_d = din("onehot", [16, S])
    out_d = nc.dram_tensor("out", [2048, D], F32, kind="ExternalOutput").ap()
    dbg_d = None
    if stage != "full":
        dbg_d = nc.dram_tensor("dbg", [2048, D], F32, kind="ExternalOutput").ap()

    cur = [16512]

    def sb(name, shape, dt, at=None):
        nbytes = int(np.prod(shape[1:])) * (4 if dt == F32 else 2)
        if at is None:
            off = cur[0]
            cur[0] = off + ((nbytes + 31) // 32) * 32
        else:
            off = at
        assert off % 32 == 0 and off + nbytes <= 229344, (name, off, nbytes)
        return nc.alloc_sbuf_tensor_at(name, list(shape), dt, offset=off)

    ident = sb("ident", [128, 128], BF16)
    cmask = sb("cmaskb", [128, 2, 256], BF16)
    scal = sb("scal", [128, 64], F32)
    subg = sb("subgs", [128, 128], F32)
    rope = sb("ropes", [128, 32, 32], F32)
    bmask = sb("bmasks", [128, 512], F32)
    osel = sb("osels", [128, 512], F32)
    lnp = sb("lnps", [128, 2, D], F32)
    omask = sb("omasks", [128, 8], F32)
    m05 = sb("m05", [128, 8], F32)
    R1 = cur[0]
    xTb = sb("xTb", [128, 8, S], BF16)
    R2 = cur[0]
    wus = [sb("wu0", [128, 8, 384], BF16), sb("wu1", [128, 8, 384], BF16)]
    KT = sb("KT", [128, S], BF16)
    QT = sb("QT", [128, 2048], BF16)
    KA = sb("KA", [128, S], BF16)
    KB = sb("KB", [128, S], BF16)
    QA = sb("QA", [128, 2048], BF16)
    QB = sb("QB", [128, 2048], BF16)
    R3 = cur[0]
    vaug = sb("vaug", [128, 32, 130], BF16)
    pvs = sb("pvs", [128, 2, 258], F32)
    stg = [sb("stg0", [128, 384], F32), sb("stg1", [128, 384], F32)]
    tmK = [sb("tmK0", [128, 4, 128], BF16), sb("tmK1", [128, 4, 128], BF16)]
    tmQ = [sb("tmQ0", [128, 4, 128], BF16), sb("tmQ1", [128, 4, 128], BF16)]
    rsA = sb("rsA", [128, 64], F32)
    rsB = sb("rsB", [128, 64], F32)
    PT = [sb("PT%d" % i, [128, 1024], BF16) for i in range(3)]
    fo = sb("fo", [128, 2, 128], F32)
    ft = sb("ft", [128, 2, 128], F32)
    fs = sb("fs", [128, 32], F32)
    gm = sb("gm", [128, 512], F32)
    gsel = sb("gsel", [128, 512], F32)
    gmx = sb("gmx", [128, 32, 8], F32)
    gthr = sb("gthr", [128, 32], F32)
    biaspad = sb("biaspad", [128, 16, 128], BF16)
    ksum = sb("ksum", [128, 16], F32)
    kmT = sb("kmT", [128, 16], BF16)
    R3end = cur[0]
    cat = sb("cat", [128, 16, D], BF16)
    R4end = cur[0]
    acc = sb("acc", [128, 16, D], F32, at=R1)
    wo_s = sb("wo_s", [128, 8, D], BF16, at=R2)
    x1T = sb("x1T", [128, 8, 2048], BF16, at=R2 + 16384)
    o3 = [R3]

    def sb3(name, shape, dt):
        nbytes = int(np.prod(shape[1:])) * (4 if dt == F32 else 2)
        off = o3[0]
        o3[0] = off + ((nbytes + 31) // 32) * 32
        assert o3[0] <= R3end, (name, o3[0], R3end)
        return nc.alloc_sbuf_tensor_at(name, list(shape), dt, offset=off)

    catT = [sb3("catT0", [128, 8, 128], BF16), sb3("catT1", [128, 8, 128], BF16)]
    yb2 = [sb3("yb", [128, D], F32), sb3("ybb", [128, D], F32)]
    x1bf2 = [sb3("x1bf", [128, D], BF16), sb3("x1bfb", [128, D], BF16)]
    lnst2 = [sb3("lnst", [128, 32], F32), sb3("lnstb", [128, 32], F32)]
    hT = [sb3("hT0", [128, 4, 512], BF16), sb3("hT1", [128, 4, 512], BF16)]
    sg = [sb3("sg0", [128, 512], F32), sb3("sg1", [128, 512], F32)]
    otmp = [sb3("otmp0", [128, 512], F32), sb3("otmp1", [128, 512], F32)]
    Gs = [sb("G0", [128, 8, 512], BF16, at=R3end), sb("G1", [128, 8, 512], BF16, at=R3end + 16384)]
    Us = [sb("U0", [128, 8, 512], BF16, at=R3end + 8192), sb("U1", [128, 8, 512], BF16, at=R3end + 24576)]
    Ws = [sb("W0", [128, 4, D], BF16, at=R2), sb("W1", [128, 4, D], BF16, at=R2 + 8192)]
    ot = [sb("ot0", [128, D], F32, at=R4end), sb("ot1", [128, D], F32, at=R4end + 4096)]
    xt = [sb("xt0", [128, D], F32, at=R4end), sb("xt1", [128, D], F32, at=R4end + 4096)]
    vaug1 = sb("vaug1", [128, 32, 130], BF16, at=R4end)
    vaugs = [vaug, vaug1]

    PT2 = [nc.alloc_psum_tensor("pq%d" % i, [128, 1024], F32) for i in range(4)]
    pstride = PT2[0][:].ap[0][0]

    sems = {e: nc.alloc_semaphore("c_" + e) for e in ENGS}
    dsem = {}

    def DS(name):
        if name not in dsem:
            dsem[name] = nc.alloc_semaphore("d_" + name)
        return dsem[name]

    def AP(t, off, dims):
        ps = t[:].ap[0][0]
        return bass.AP(t, off, [[ps, dims[0]]] + [list(d) for d in dims[1:]])

    def APp(t, p0, off, dims):
        base = t[p0:p0 + dims[0]]
        return bass.AP(base.tensor, base.offset + off, [[base.ap[0][0], dims[0]]] + [list(d) for d in dims[1:]])

    def dma(eng, out, in_, sem, reads=(), writes=()):
        return P.op(eng, lambda e: e.dma_start(out=out, in_=in_), reads=reads, writes=writes, dma_sem=DS(sem))

    dma("sp", rope[:].rearrange("p a b -> p (a b)"), rope_d, "c0", writes=["rope"])
    dma("sp", bmask[:], bmask_d, "c1", writes=["bmask"])
    dma("sp", osel[:], osel_d, "c2", writes=["osel"])
    dma("sp", omask[:, 0:1], omask_d, "c3", writes=["omask"])
    dma("sp", subg[:], subg_d, "c4", writes=["subg"])
    dma("sp", gm[:, 0:256], lam_d, "c5", writes=["gm"])
    dma("sp", lnp[:, 0, :], lnp_d[0], "c6", writes=["lnp0"])
    dma("sp", lnp[:, 1, :], lnp_d[1], "c7", writes=["lnp1"])
    xT_v = xT_d.rearrange("(kc p) t -> p kc t", p=128)
    wu_v = wu_d.rearrange("u (kc p) c -> u p kc c", p=128)

    def load_wu(pos):
        dma("pool", wus[pos % 2][:], wu_v[[0, 4, 1, 5, 2, 6, 3, 7][pos]], "wu%d" % (pos % 2), writes=[("wu", pos % 2)])

    def load_xT(blk):
        dma("pool", xTb[:, :, blk * 512:(blk + 1) * 512], xT_v[:, :, blk * 512:(blk + 1) * 512],
            "xT%d" % blk, writes=[("xT", blk)])

    load_wu(0)
    load_xT(0)
    dma("pool", ident[:], ident_d, "c8", writes=["ident"])
    dma("pool", cmask[:].rearrange("p a b -> p (a b)"), cmask_d, "c9", writes=["cmask"])
    load_xT(1)
    load_wu(1)
    for blk in range(2, 8):
        load_xT(blk)
    for nm, t in (("KA", KA), ("KB", KB), ("QA", QA), ("QB", QB)):
        keys = [(nm, i) for i in range(8)] + [(nm + "a", i) for i in range(2)]
        P.op("pool", lambda e, t=t: e.memset(t[:], 0.0), writes=keys)
    P.op("pool", lambda e: e.memset(biaspad[:], 0.0), writes=["biaspad"])
    for vs_ in range(2):
        P.op("pool", lambda e, vs_=vs_: e.memset(vaugs[vs_][:], 1.0), writes=[("V", vs_, t_) for t_ in range(32)])
    P.op("pool", lambda e: e.memset(m05[:], -0.5), writes=["m05"])
    for hh in range(4):
        dma("pool", KA[64:80, hh * 1024:(hh + 1) * 1024], onehot_d[:, hh * 1024:(hh + 1) * 1024], "c10", writes=[("KA", i) for i in range(8)])
        dma("pool", KB[32:48, hh * 1024:(hh + 1) * 1024], onehot_d[:, hh * 1024:(hh + 1) * 1024], "c11", writes=[("KB", i) for i in range(8)])

    lamv = gm[:, 0:256].rearrange("p (a b) -> p a b", a=4)
    P.op("dve", lambda e: e.tensor_tensor(gsel[:, 0:128].rearrange("p (a b) -> p a b", a=2),
                                          lamv[:, 0:4:2, :], lamv[:, 1:4:2, :], ALU.mult),
         reads=["gm"], writes=["gsel"])
    P.op("dve", lambda e: e.tensor_reduce(scal[:, 0:2], gsel[:, 0:128].rearrange("p (a b) -> p a b", a=2), AX.X, ALU.add),
         reads=["gsel"], writes=["scal01"])
    P.op("act", lambda e: e.activation(scal[:, 2:4], scal[:, 0:2], AF.Exp), reads=["scal01"], writes=["scal23"])
    P.op("dve", lambda e: e.tensor_tensor(scal[:, 4:5], scal[:, 3:4], scal[:, 2:3], ALU.subtract),
         reads=["scal23"], writes=["scal4"])
    P.op("dve", lambda e: e.tensor_scalar(scal[:, 5:6], scal[:, 4:5], -LAMBDA_INIT, None, ALU.add),
         reads=["scal4"], writes=["neglam"])
    neglam = scal[:, 5:6]
    P.op("dve", lambda e: e.tensor_scalar(subg[:], subg[:], 1.0 - LAMBDA_INIT, None, ALU.mult),
         reads=["subg"], writes=["subg"])

    T0 = PT2[0]
    PJ = T0[:, 0:512]
    TP = T0[:, 512:1024]
    TPb = T0[:].bitcast(BF16)[:, 1024:2048]
    STt = [PT2[1], PT2[2]]
    PVts = [PT2[3], PT2[3]]
    PVbk = [(6, 7), (6, 7)]
    vaug5s = [v_[:].rearrange("p t (h c) -> p t h c", h=2) for v_ in vaugs]
    ORDER = [0, 4, 1, 5, 2, 6, 3, 7]

    KP = int(os.environ.get("K_PSTEP", "9"))

    def copy_op(eng, dst, src, reads, writes):
        if eng == "act":
            P.op("act", lambda e: e.activation(dst, src, AF.Copy), reads=reads, writes=writes)
        else:
            P.op("dve", lambda e: e.tensor_copy(dst, src), reads=reads, writes=writes)

    def project_unit(pos):
        u = ORDER[pos]
        moba = u >= 4
        w = wus[pos % 2]
        wk = ("wu", pos % 2)
        vs_ = pos % 2
        vaug_s, vaug5_s = vaugs[vs_], vaug5s[vs_]
        cpe = "act" if pos == 0 else "dve"
        jobi = [0]
        if pos == 0:
            PJB = [(PT2[0], 0, ("BK", 0)), (PT2[1], 0, ("BK", 2)), (PT2[1], 512, ("BK", 3)),
                   (PT2[2], 0, ("BK", 4)), (PT2[2], 512, ("BK", 5))]
        else:
            PJB = [(PT2[0], 0, ("BK", 0)), (PT2[0], 512, ("BK", 1))]

        def job(tt, c0, ncols):
            h = PJB[jobi[0] % len(PJB)]
            jobi[0] += 1
            pj = h[0][:, h[1]:h[1] + ncols]
            for kc in range(8):
                P.op("pe", lambda e, kc=kc, pj=pj: e.matmul(pj, xTb[:, kc, tt * 128:(tt + 1) * 128],
                                                            w[:, kc, c0:c0 + ncols], start=(kc == 0), stop=(kc == 7)),
                     reads=[("xT", tt // 4), wk], writes=[h[2]])
            si = jobi[0] % 2
            P.op("dve", lambda e: e.tensor_copy(stg[si][:, 0:ncols], pj), reads=[h[2]], writes=[("stg", si)])
            return (stg[si], 0, ("stg", si))

        def rope_job(h, tt, nmaps, dsts):
            base = h[1]
            PJT = h[0]
            hk = h[2]
            xs = AP(PJT, base, [128, [64, nmaps], [1, 16]])
            xsw = AP(PJT, base + 8, [128, [64, nmaps], [-8, 2], [1, 8]])
            cc = AP(rope, tt * 32, [128, [0, nmaps], [1, 16]])
            ss = AP(rope, tt * 32 + 16, [128, [0, nmaps], [8, 2], [1, 8]])
            a_o = AP(rsA, 0, [128, [16, nmaps], [1, 16]])
            b_o = AP(rsB, 0, [128, [16, nmaps], [8, 2], [1, 8]])
            P.op("dve", lambda e: e.tensor_tensor(a_o, xs, cc, ALU.mult), reads=[hk, "rope"], writes=["rsA"])
            P.op("dve", lambda e: e.tensor_tensor(b_o, xsw, ss, ALU.mult), reads=[hk, "rope"], writes=["rsB"])
            for i, (dt_, key, doff) in enumerate(dsts):
                a_i = AP(rsA, i * 32, [128, [16, 2], [1, 16]])
                b_i = AP(rsB, i * 32, [128, [16, 2], [1, 16]])
                d_r = AP(dt_, doff, [128, [64, 2], [1, 16]])
                d_c = AP(dt_, doff + 16, [128, [64, 2], [1, 48]])
                src = AP(PJT, base + i * 128 + 16, [128, [64, 2], [1, 48]])
                P.op("dve", lambda e, d_r=d_r, a_i=a_i, b_i=b_i: e.tensor_tensor(d_r, a_i, b_i, ALU.add),
                     reads=["rsA", "rsB"], writes=[key])
                copy_op(cpe, d_c, src, [hk], [key])

        def vcopy(h, c0, tt):
            src = h[0][:, h[1] + c0:h[1] + c0 + 128]
            if not moba:
                dst = vaug_s[:, tt, 0:128]
            else:
                dst = vaug5_s[:, tt, :, 0:64]
                src = src.rearrange("p (h c) -> p h c", h=2)
            copy_op(cpe, dst, src, [h[2]], [("V", vs_, tt)])

        for blk in range(8):
            own = blk < 4
            sl = blk % 2
            for t4 in range(4):
                tt = blk * 4 + t4
                kkey = ("tmK", sl, t4)
                qkey = ("tmQ", sl, t4)
                if own:
                    h = job(tt, 0, 384)
                    rope_job(h, tt, 4, [(tmQ[sl], qkey, t4 * 128), (tmK[sl], kkey, t4 * 128)])
                    vcopy(h, 256, tt)
                else:
                    h = job(tt, 128, 256)
                    rope_job(h, tt, 2, [(tmK[sl], kkey, t4 * 128)])
                    vcopy(h, 128, tt)
                yield
            for t4 in range(4):
                P.op("pe", lambda e, t4=t4: e.transpose(TPb[:, t4 * 128:(t4 + 1) * 128], tmK[sl][:, t4, :], ident[:]),
                     reads=[("tmK", sl, t4), "ident"], writes=[("BK", 1)])
            if own:
                for t4 in range(4):
                    P.op("pe", lambda e, t4=t4: e.transpose(TPb[:, 512 + t4 * 128:512 + (t4 + 1) * 128], tmQ[sl][:, t4, :], ident[:]),
                         reads=[("tmQ", sl, t4), "ident"], writes=[("BK", 1)])
            c0, c1 = blk * 512, (blk + 1) * 512
            if not moba:
                P.op("dve", lambda e: e.tensor_copy(KT[:, c0:c1], TPb[:, 0:512]), reads=[("BK", 1)], writes=[("KT", blk)])
                if own:
                    P.op("dve", lambda e: e.tensor_copy(QT[:, c0:c1], TPb[:, 512:1024]), reads=[("BK", 1)], writes=[("QT", blk)])
            else:
                P.op("dve", lambda e: e.tensor_copy(KA[0:64, c0:c1], TPb[0:64, 0:512]), reads=[("BK", 1)], writes=[("KA", blk)])
                P.op("dve", lambda e: e.tensor_copy(KB[64:128, c0:c1], TPb[64:128, 0:512]), reads=[("BK", 1)], writes=[("KB", blk)])
                if own:
                    P.op("dve", lambda e: e.tensor_copy(QA[0:64, c0:c1], TPb[0:64, 512:1024]), reads=[("BK", 1)], writes=[("QA", blk)])
                    P.op("dve", lambda e: e.tensor_copy(QB[64:128, c0:c1], TPb[64:128, 512:1024]), reads=[("BK", 1)], writes=[("QB", blk)])
            yield

    def moba_gate(u):
        kall = [("KA", i) for i in range(8)]
        kbll = [("KB", i) for i in range(8)]
        P.op("dve", lambda e: e.tensor_reduce(ksum[0:64, :], KA[0:64, :].rearrange("p (n k) -> p n k", n=16), AX.X, ALU.add),
             reads=kall, writes=["ksumA"])
        P.op("dve", lambda e: e.tensor_reduce(ksum[64:128, :], KB[64:128, :].rearrange("p (n k) -> p n k", n=16), AX.X, ALU.add),
             reads=kbll, writes=["ksumB"])
        P.op("dve", lambda e: e.tensor_scalar(kmT[:], ksum[:], 1.0 / 256.0, None, ALU.mult),
             reads=["ksumA", "ksumB"], writes=["kmT"])
        for _ in range(6):
            yield
        for qt in range(16):
            P.op("pe", lambda e, qt=qt: e.matmul(PJ[:, qt * 16:qt * 16 + 16], QA[0:64, qt * 128:(qt + 1) * 128], kmT[0:64, :],
                                                 start=True, stop=True),
                 reads=[("QA", qt // 4), "kmT"], writes=[("BK", 0)])
            P.op("pe", lambda e, qt=qt: e.matmul(TP[:, qt * 16:qt * 16 + 16], QB[64:128, qt * 128:(qt + 1) * 128], kmT[64:128, :],
                                                 start=True, stop=True),
                 reads=[("QB", qt // 4), "kmT"], writes=[("BK", 1)])
        P.op("dve", lambda e: e.tensor_tensor(gm[:, 0:256], PJ[:, 0:256], bmask[:, 0:256], ALU.add),
             reads=[("BK", 0), "bmask"], writes=["gm"])
        P.op("dve", lambda e: e.tensor_tensor(gm[:, 256:512], TP[:, 0:256], bmask[:, 256:512], ALU.add),
             reads=[("BK", 1), "bmask"], writes=["gm"])
        for g in range(32):
            P.op("dve", lambda e, g=g: e.max(gmx[:, g, :], gm[:, g * 16:(g + 1) * 16]), reads=["gm"], writes=["gmx"])
        P.op("dve", lambda e: e.tensor_scalar(gthr[:], gmx[:, :, 2], NEG / 2, None, ALU.max), reads=["gmx"], writes=["gthr"])
        gm3 = gm[:].rearrange("p (g n) -> p g n", g=32)
        gs3 = gsel[:].rearrange("p (g n) -> p g n", g=32)
        thr_b = AP(gthr, 0, [128, [1, 32], [0, 16]])
        P.op("dve", lambda e: e.tensor_tensor(gs3, gm3, thr_b, ALU.is_ge), reads=["gm", "gthr"], writes=["gsel"])
        P.op("dve", lambda e: e.tensor_tensor(gsel[:], gsel[:], osel[:], ALU.add), reads=["gsel", "osel"], writes=["gsel"])
        gs4 = gsel[:].rearrange("p (h q n) -> p h q n", q=16, h=2)
        P.op("dve", lambda e: e.tensor_scalar(biaspad[:, :, 64:80], gs4[:, 0, :, :], -1.0, -NEG, ALU.add, ALU.mult),
             reads=["gsel"], writes=["biaspad"])
        P.op("dve", lambda e: e.tensor_scalar(biaspad[:, :, 32:48], gs4[:, 1, :, :], -1.0, -NEG, ALU.add, ALU.mult),
             reads=["gsel"], writes=["biaspad"])
        for _ in range(6):
            yield
        for half in range(2):
            for q8 in range(8):
                qt = half * 8 + q8
                P.op("pe", lambda e, qt=qt, q8=q8: e.transpose(TPb[:, q8 * 128:(q8 + 1) * 128], biaspad[:, qt, :], ident[:]),
                     reads=["biaspad", "ident"], writes=[("BK", 1)])
            c0, c1 = half * 1024, (half + 1) * 1024
            P.op("dve", lambda e, c0=c0, c1=c1: e.tensor_copy(QA[64:80, c0:c1], TPb[64:80, :]), reads=[("BK", 1)], writes=[("QAa", half)])
            P.op("dve", lambda e, c0=c0, c1=c1: e.tensor_copy(QB[32:48, c0:c1], TPb[32:48, :]), reads=[("BK", 1)], writes=[("QBa", half)])

    grp_ctr = [0]

    def attention_unit(pos, filler, n_fill_=40):
        u = ORDER[pos]
        moba = u >= 4
        vs_ = pos % 2
        vaug_s, vaug5_s = vaugs[vs_], vaug5s[vs_]
        n_fill = n_fill_ if filler is not None else 0
        fill_done = [0]
        items = []
        nch = int(os.environ.get("K_CH", "8"))
        for j in range(nch):
            gl = [("own", i) for i in range(j + 1)] + [("oth", i) for i in range(j + 1)]
            for gi_, (kind, i) in enumerate(gl):
                items.append(dict(j=j, kind=kind, i=i, first=(gi_ == 0), last=(gi_ == len(gl) - 1)))

        def qk_exp(it):
            j, kind, i = it["j"], it["kind"], it["i"]
            gi = grp_ctr[0]
            grp_ctr[0] += 1
            it["gi"] = gi
            qc0 = j * 256
            stt = STt[gi % 2]
            sb_ = (stt[:, 0:512], stt[:, 512:1024])
            skeys = [("BK", 2 + 2 * (gi % 2)), ("BK", 3 + 2 * (gi % 2))]
            pt = PT[gi % 3]
            pkey = ("PT", gi % 3)
            blk = i if kind == "own" else 8 + i
            diag = (kind == "own" and i == j)
            it["blk"], it["diag"] = blk, diag
            for kt in range(2):
                kc0 = blk * 256 + kt * 128
                for m in range(2):
                    o_ap = sb_[m][:, kt * 256:(kt + 1) * 256]
                    if not moba:
                        lh = KT[m * 64:(m + 1) * 64, kc0:kc0 + 128]
                        rh = QT[m * 64:(m + 1) * 64, qc0:qc0 + 256]
                        rk = [("KT", blk // 2), ("QT", j // 2)]
                    else:
                        kt_, qt_ = (KA, QA) if m == 0 else (KB, QB)
                        nm = "A" if m == 0 else "B"
                        lh = kt_[:, kc0:kc0 + 128]
                        rh = qt_[:, qc0:qc0 + 256]
                        rk = [("K" + nm, blk // 2), ("Q" + nm, j // 2), ("Q" + nm + "a", j // 4)]
                    P.op("pe", lambda e, o_ap=o_ap, lh=lh, rh=rh: e.matmul(o_ap, lh, rh, start=True, stop=not diag),
                         reads=rk, writes=skeys)
                    if diag:
                        P.op("pe", lambda e, o_ap=o_ap, kt=kt: e.matmul(o_ap, ident[:], cmask[:, kt, :], start=False, stop=True),
                             reads=["ident", "cmask"], writes=skeys)
            s_in = stt[:].rearrange("p (m c) -> p m c", m=2)
            use_om = (not moba) and kind == "oth" and i == j
            if use_om:
                P.op("act", lambda e, s_in=s_in, pt=pt: e.activation(pt[:].rearrange("p (m c) -> p m c", m=2), s_in, AF.Exp,
                                                                   bias=omask[:, 0:1], scale=SCALE),
                     reads=skeys + ["omask"], writes=[pkey])
            else:
                P.op("act", lambda e, s_in=s_in, pt=pt: e.activation(pt[:].rearrange("p (m c) -> p m c", m=2), s_in, AF.Exp,
                                                                   scale=SCALE),
                     reads=skeys, writes=[pkey])

        def pv(it):
            j, gi, blk, diag = it["j"], it["gi"], it["blk"], it["diag"]
            pt = PT[gi % 3]
            pkey = ("PT", gi % 3)
            PVt = PVts[j % 2]
            PV = [PVt[:, 0:512], PVt[:, 512:1024]]
            bks = PVbk[j % 2]
            for kt in range(2):
                tok_tile = blk * 2 + kt
                for m in range(2):
                    for qt in range(2):
                        if diag and kt == 1 and qt == 0:
                            continue
                        lh = pt[:, m * 512 + kt * 256 + qt * 128: m * 512 + kt * 256 + (qt + 1) * 128]
                        if not moba:
                            rh = vaug_s[:, tok_tile, 0:129]
                            o_ap = PV[m][:, qt * 129:(qt + 1) * 129]
                        else:
                            rh = vaug5_s[:, tok_tile, m, :]
                            o_ap = PV[m][:, qt * 65:(qt + 1) * 65]
                        st = it["first"] and kt == 0 and qt == 0
                        P.op("pe", lambda e, o_ap=o_ap, lh=lh, rh=rh, st=st: e.matmul(o_ap, lh, rh, start=st, stop=False,
                                                                                       skip_group_check=True),
                             reads=[pkey, ("V", vs_, tok_tile)], writes=[("BK", bks[m])])
            if it["last"]:
                finalize_chunk(u, j)

        for idx in range(len(items) + 1):
            if idx < len(items):
                qk_exp(items[idx])
            if idx >= 1:
                pv(items[idx - 1])
            if filler is not None:
                due = min(n_fill, ((idx + 1) * n_fill + len(items) - 1) // len(items))
                while fill_done[0] < due:
                    next(filler, None)
                    fill_done[0] += 1
        if filler is not None:
            for _ in filler:
                pass

    def finalize_chunk(u, j):
        moba = u >= 4
        bks = PVbk[j % 2]
        ncol = 130 if moba else 258
        for m_i in range(2):
            P.op("dve", lambda e, m_i=m_i: e.tensor_copy(pvs[:, m_i, 0:ncol], PVts[0][:, m_i * 512:m_i * 512 + ncol]),
                 reads=[("BK", bks[m_i])], writes=[("pvs", m_i)])
        PVt = pvs
        PV = [pvs[:, 0, :], pvs[:, 1, :]]
        if not moba:
            l0 = AP(pvs, 128, [128, [129, 2]])
            l1 = AP(pvs, 258 + 128, [128, [129, 2]])
            P.op("dve", lambda e: e.reciprocal(fs[:, 0:2], l0), reads=[("pvs", 0)], writes=["fs01"])
            P.op("dve", lambda e: e.reciprocal(fs[:, 2:4], l1), reads=[("pvs", 1)], writes=["fs23"])
            P.op("dve", lambda e: e.tensor_scalar(fs[:, 4:6], fs[:, 2:4], neglam, None, ALU.mult),
                 reads=["fs23", "neglam"], writes=["fs45"])
            for qt in range(2):
                P.op("dve", lambda e, qt=qt: e.tensor_scalar(ft[:, qt, :], PV[1][:, qt * 129:qt * 129 + 128], fs[:, 4 + qt:5 + qt], None, ALU.mult),
                     reads=[("pvs", 1), "fs45"], writes=[("ft", qt)])
                P.op("dve", lambda e, qt=qt: e.scalar_tensor_tensor(fo[:, qt, :], PV[0][:, qt * 129:qt * 129 + 128], fs[:, qt:qt + 1],
                                                                      ft[:, qt, :], ALU.mult, ALU.add),
                     reads=[("pvs", 0), "fs01", ("ft", qt)], writes=[("fo", qt)])
                P.op("dve", lambda e, qt=qt: e.scalar_tensor_tensor(ft[:, qt, :], fo[:, qt, :], 1.0, fo[:, qt, :], ALU.mult, ALU.mult,
                                                                      accum_out=fs[:, 8 + qt:9 + qt]),
                     reads=[("fo", qt)], writes=[("ft", qt), ("fss", qt)])
            P.op("dve", lambda e: e.tensor_scalar(fs[:, 12:14], fs[:, 8:10], 1.0 / 128.0, EPS, ALU.mult, ALU.add),
                 reads=[("fss", 0), ("fss", 1)], writes=["fsv"])
            P.op("pool", lambda e: e.tensor_tensor(fs[:, 16:18], fs[:, 12:14], m05[:, 0:2], ALU.pow),
                 reads=["fsv", "m05"], writes=["fsr"])
            for qt in range(2):
                tile = 2 * j + qt
                P.op("dve", lambda e, qt=qt, tile=tile: e.scalar_tensor_tensor(cat[:, tile, u * 128:(u + 1) * 128], fo[:, qt, :],
                                                                                fs[:, 16 + qt:17 + qt], subg[:], ALU.mult, ALU.mult),
                     reads=[("fo", qt), "fsr", "subg"], writes=[("cat", tile, u)])
        else:
            m_ = u - 4
            for h in range(2):
                lh = AP(pvs, h * 258 + 64, [128, [65, 2]])
                P.op("dve", lambda e, h=h, lh=lh: e.reciprocal(fs[:, 20 + 2 * h:22 + 2 * h], lh), reads=[("pvs", h)], writes=[("fsm", h)])
                for qt in range(2):
                    tile = 2 * j + qt
                    c0 = 512 + (2 * m_ + h) * 64
                    P.op("dve", lambda e, h=h, qt=qt, tile=tile, c0=c0: e.tensor_scalar(cat[:, tile, c0:c0 + 64],
                                                                                         PV[h][:, qt * 65:qt * 65 + 64],
                                                                                         fs[:, 20 + 2 * h + qt:21 + 2 * h + qt], None, ALU.mult),
                         reads=[("pvs", h), ("fsm", h)], writes=[("cat", tile, u)])

    wo_v = wo_d.rearrange("(ec p) d -> p ec d", p=128)
    g0 = project_unit(0)
    for _ in g0:
        pass
    load_wu(2)
    for pos in range(8):
        u = ORDER[pos]
        nxt, nf = None, 40
        if pos + 1 < 8:
            if ORDER[pos + 1] >= 4:
                def chain_(a_, b_):
                    for _ in a_:
                        yield
                    for _ in b_:
                        yield
                nxt, nf = chain_(project_unit(pos + 1), moba_gate(ORDER[pos + 1])), 40 + 13
            else:
                nxt = project_unit(pos + 1)
        if pos == 7:
            ovl = [("wu", 0), ("wu", 1)] + [("KT", b_) for b_ in range(8)]
            for ec in range(8):
                dma("pool", wo_s[:, ec, :], wo_v[:, ec, :], "wo", writes=["wo"] + ovl)
        attention_unit(pos, nxt, nf)
        if pos + 3 < 8:
            load_wu(pos + 3)

    P.barrier()

    if stage == "A":
        dv = dbg_d.rearrange("(t p) d -> p t d", p=128)
        for t in range(16):
            o = dma("pool", dv[:, t, :], cat[:, t, :], "dbg", reads=[("cat", t, uu) for uu in range(8)])
            P.out_dmas.append(o)
        ov = out_d.rearrange("(t p) d -> p t d", p=128)
        P.finalize(sems)
        return nc

    TPc = T0[:].bitcast(BF16)[:, 0:1024]
    TPx = T0[:].bitcast(BF16)[:, 1024:2048]
    MIX = [(PT2[1][:, 0:512], PT2[1][:, 512:1024]), (PT2[2][:, 0:512], PT2[2][:, 512:1024])]
    MIXt = [PT2[1], PT2[2]]
    xown_v = xown_d.rearrange("(t p) d -> p t d", p=128)

    lnB = sb("lnB", [128, 16, 32], F32, at=R4end + 8192)

    def ln_stats(src_ap, skey, s_):
        lnst = lnst2[s_] if isinstance(s_, int) else lnB[:, s_[1], :]
        P.op("dve", lambda e: e.bn_stats(lnst[:, 0:6], src_ap[:, 0:512]), reads=[skey], writes=[("lnst_a", s_)])
        P.op("dve", lambda e: e.bn_stats(lnst[:, 6:12], src_ap[:, 512:1024]), reads=[skey], writes=[("lnst_b", s_)])
        P.op("dve", lambda e: e.bn_aggr(lnst[:, 12:14], lnst[:, 0:12]), reads=[("lnst_a", s_), ("lnst_b", s_)], writes=[("lnmv", s_)])
        P.op("dve", lambda e: e.tensor_scalar(lnst[:, 14:15], lnst[:, 13:14], EPS, None, ALU.add), reads=[("lnmv", s_)], writes=[("lnve", s_)])
        P.op("pool", lambda e: e.tensor_tensor(lnst[:, 16:17], lnst[:, 14:15], m05[:, 0:1], ALU.pow), reads=[("lnve", s_), "m05"], writes=[("lnr", s_)])
        P.op("dve", lambda e: e.scalar_tensor_tensor(lnst[:, 17:18], lnst[:, 12:13], -1.0, lnst[:, 16:17], ALU.mult, ALU.mult),
             reads=[("lnmv", s_), ("lnr", s_)], writes=[("lnnmr", s_)])

    def B1(t):
        sl = t % 2
        dma("sp", xt[sl][:], xown_v[:, t, :], "xt%d" % sl, writes=[("xt", sl)])
        for ec in range(8):
            P.op("pe", lambda e, ec=ec: e.transpose(TPc[:, ec * 128:(ec + 1) * 128], cat[:, t, ec * 128:(ec + 1) * 128], ident[:]),
                 reads=[("cat", t, ec), "ident"], writes=["TPc"])
        P.op("act", lambda e: e.activation(catT[sl][:].rearrange("p a b -> p (a b)"), TPc[:, :], AF.Copy),
             reads=["TPc"], writes=[("catT", sl)])
        mx = MIX[sl]
        for dh in range(2):
            for ec in range(8):
                P.op("pe", lambda e, ec=ec, dh=dh: e.matmul(mx[dh], catT[sl][:, ec, :], wo_s[:, ec, dh * 512:(dh + 1) * 512],
                                                            start=(ec == 0), stop=(ec == 7)),
                     reads=[("catT", sl), "wo"], writes=[("MIX", sl)])
        mix_ap = MIXt[sl][:].rearrange("p (a b) -> p a b", a=2)
        P.op("dve", lambda e: e.scalar_tensor_tensor(acc[:, t, :].rearrange("p (a b) -> p a b", a=2),
                                                     xt[sl][:].rearrange("p (a b) -> p a b", a=2), ALPHA, mix_ap,
                                                     ALU.mult, ALU.add),
             reads=[("xt", sl), ("MIX", sl)], writes=[("acc", t)])
        ln_stats(acc[:, t, :], ("acc", t), ("B", t))

    def B2dve(t):
        s_ = ("B", t)
        P.op("dve", lambda e: e.tensor_scalar(acc[:, t, :], acc[:, t, :], lnB[:, t, 16:17], lnB[:, t, 17:18], ALU.mult, ALU.add),
             reads=[("acc", t), ("lnr", s_), ("lnnmr", s_)], writes=[("acc", t)])
        P.op("dve", lambda e: e.tensor_tensor(acc[:, t, :], acc[:, t, :], lnp[:, 0, :], ALU.mult), reads=[("acc", t), "lnp0"], writes=[("acc", t)])
        P.op("dve", lambda e: e.tensor_tensor(acc[:, t, :], acc[:, t, :], lnp[:, 1, :], ALU.add), reads=[("acc", t), "lnp1"], writes=[("acc", t)])

    def B2rest(t):
        sl = t % 2
        x1bf = x1bf2[sl]
        P.op("act", lambda e: e.activation(x1bf[:], acc[:, t, :], AF.Copy), reads=[("acc", t)], writes=[("x1bf", sl)])
        for kc in range(8):
            P.op("pe", lambda e, kc=kc: e.transpose(TPx[:, kc * 128:(kc + 1) * 128], x1bf[:, kc * 128:(kc + 1) * 128], ident[:]),
                 reads=[("x1bf", sl), "ident"], writes=["TPx"])
        P.op("act", lambda e: e.activation(x1T[:, :, t * 128:(t + 1) * 128], TPx[:, :].rearrange("p (a b) -> p a b", a=8), AF.Copy),
             reads=["TPx"], writes=[("x1T", t // 4)])
        P.op("pool", lambda e: e.tensor_scalar(acc[:, t, :], acc[:, t, :], ALPHA, 1.0, ALU.mult, ALU.mult),
             reads=[("acc", t)], writes=[("acc", t)])

    wg_v = wg_d.rearrange("(kc p) f -> p kc f", p=128)
    wup_v = wup_d.rearrange("(kc p) f -> p kc f", p=128)
    wout_v = wout_d.rearrange("(c p) d -> p c d", p=128)
    groups = ffn_groups()

    def cat_keys(t0, t1):
        return [("cat", t_, e_) for t_ in range(t0, t1) for e_ in range(8)]

    def load_ffn(gi, which="GUW", ag=(), au=(), aw=()):
        c0, n = groups[gi]
        s_ = gi % 2
        if "G" in which:
            dma("pool", Gs[s_][:, :, 0:n * 128], wg_v[:, :, c0 * 128:(c0 + n) * 128], "G%d" % s_, writes=[("G", s_)] + list(ag))
        if "U" in which:
            dma("pool", Us[s_][:, :, 0:n * 128], wup_v[:, :, c0 * 128:(c0 + n) * 128], "U%d" % s_, writes=[("U", s_)] + list(au))
        if "W" in which:
            dma("pool", Ws[s_][:, 0:n, :], wout_v[:, c0:c0 + n, :], "W%d" % s_, writes=[("W", s_)] + list(aw))

    for t in range(16 + 2):
        if t == 10:
            load_ffn(0, "GU", ag=cat_keys(0, 4), au=cat_keys(4, 8))
        if t - 2 >= 0:
            B2dve(t - 2)
        if t < 16:
            B1(t)
        if t - 2 >= 0:
            B2rest(t - 2)

    if stage == "B":
        dv = dbg_d.rearrange("(t p) d -> p t d", p=128)
        for t in range(16):
            o = dma("sp", dv[:, t, :], acc[:, t, :], "dbg", reads=[("acc", t)])
            P.out_dmas.append(o)
        P.finalize(sems)
        return nc

    load_ffn(0, "W", aw=["wo"])
    P.barrier()
    load_ffn(1, "GUW", ag=cat_keys(8, 12), au=cat_keys(12, 16), aw=["wo"])
    dma("sp", lnp[:, 0, :], lnp_d[2], "c6", writes=["lnp0"])
    dma("sp", lnp[:, 1, :], lnp_d[3], "c7", writes=["lnp1"])
    GB = [PT2[0][:, 0:512], PT2[0][:, 512:1024]]
    UB = [PT2[1][:, 0:512], PT2[1][:, 512:1024]]
    OB = [PT2[2][:, 0:512], PT2[2][:, 512:1024]]
    ov = out_d.rearrange("(t p) d -> p t d", p=128)

    def D1a(t):
        s_ = ("B", t)
        lnst = lnB[:, t, :]
        src_ap = acc[:, t, :]
        P.op("dve", lambda e: e.bn_stats(lnst[:, 0:6], src_ap[:, 0:512]), reads=[("acc", t)], writes=[("lnst_a", s_)])
        P.op("dve", lambda e: e.bn_stats(lnst[:, 6:12], src_ap[:, 512:1024]), reads=[("acc", t)], writes=[("lnst_b", s_)])
        P.op("dve", lambda e: e.bn_aggr(lnst[:, 12:14], lnst[:, 0:12]), reads=[("lnst_a", s_), ("lnst_b", s_)], writes=[("lnmv", s_)])
        P.op("dve", lambda e: e.tensor_scalar(lnst[:, 14:15], lnst[:, 13:14], EPS, None, ALU.add), reads=[("lnmv", s_)], writes=[("lnve", s_)])
        P.op("pool", lambda e: e.tensor_tensor(lnst[:, 16:17], lnst[:, 14:15], m05[:, 0:1], ALU.pow), reads=[("lnve", s_), "m05"], writes=[("lnr", s_)])

    def D1b(t):
        s_ = ("B", t)
        lnst = lnB[:, t, :]
        P.op("dve", lambda e: e.scalar_tensor_tensor(lnst[:, 17:18], lnst[:, 12:13], -1.0, lnst[:, 16:17], ALU.mult, ALU.mult),
             reads=[("lnmv", s_), ("lnr", s_)], writes=[("lnnmr", s_)])

    def D2act(t):
        sl = t % 2
        yb, lnst = yb2[sl], lnB[:, t, :]
        P.op("act", lambda e: e.activation(yb[:], acc[:, t, :], AF.Identity, bias=lnst[:, 17:18], scale=lnst[:, 16:17]),
             reads=[("acc", t), ("lnr", ("B", t)), ("lnnmr", ("B", t))], writes=[("yb", sl)])

    def D2dve(t):
        sl = t % 2
        yb = yb2[sl]
        P.op("dve", lambda e: e.tensor_tensor(yb[:], yb[:], lnp[:, 0, :], ALU.mult), reads=[("yb", sl), "lnp0"], writes=[("yb", sl)])
        P.op("pool", lambda e: e.tensor_tensor(ot[sl][:], yb[:], lnp[:, 1, :], ALU.add), reads=[("yb", sl), "lnp1"], writes=[("ot", sl)])
        o = dma("sp", ov[:, t, :], ot[sl][:], "ot%d" % sl, reads=[("ot", sl)])
        P.out_dmas.append(o)

    def D1(t):
        D1a(t)
        D1b(t)

    def D2(t):
        sl = t % 2
        yb, lnst = yb2[sl], lnB[:, t, :]
        P.op("act", lambda e: e.activation(yb[:], acc[:, t, :], AF.Identity, bias=lnst[:, 17:18], scale=lnst[:, 16:17]),
             reads=[("acc", t), ("lnr", ("B", t)), ("lnnmr", ("B", t))], writes=[("yb", sl)])
        P.op("dve", lambda e: e.tensor_tensor(yb[:], yb[:], lnp[:, 0, :], ALU.mult), reads=[("yb", sl), "lnp0"], writes=[("yb", sl)])
        P.op("dve", lambda e: e.tensor_tensor(ot[sl][:], yb[:], lnp[:, 1, :], ALU.add), reads=[("yb", sl), "lnp1"], writes=[("ot", sl)])
        o = dma("sp", ov[:, t, :], ot[sl][:], "ot%d" % sl, reads=[("ot", sl)])
        P.out_dmas.append(o)

    ci = [0]
    oi = [0]
    for gi, (c0, n) in enumerate(groups):
        s_ = gi % 2
        for tb in range(4):
            hs = (gi * 4 + tb) % 2
            for c in range(n):
                b_ = ci[0] % 2
                ci[0] += 1
                for kc in range(8):
                    P.op("pe", lambda e, kc=kc, c=c, b_=b_, s_=s_, tb=tb: e.matmul(GB[b_], Gs[s_][:, kc, c * 128:(c + 1) * 128],
                                                                               x1T[:, kc, tb * 512:(tb + 1) * 512], start=(kc == 0), stop=(kc == 7)),
                         reads=[("G", s_), ("x1T", tb)], writes=[("GB", b_)])
                for kc in range(8):
                    P.op("pe", lambda e, kc=kc, c=c, b_=b_, s_=s_, tb=tb: e.matmul(UB[b_], Us[s_][:, kc, c * 128:(c + 1) * 128],
                                                                               x1T[:, kc, tb * 512:(tb + 1) * 512], start=(kc == 0), stop=(kc == 7)),
                         reads=[("U", s_), ("x1T", tb)], writes=[("UB", b_)])
                P.op("act", lambda e, b_=b_: e.activation(sg[b_][:], GB[b_], AF.Silu), reads=[("GB", b_)], writes=[("sg", b_)])
                P.op("dve", lambda e, b_=b_, c=c, hs=hs: e.tensor_tensor(hT[hs][:, c, :], sg[b_][:], UB[b_], ALU.mult),
                     reads=[("sg", b_), ("UB", b_)], writes=[("hT", hs)])
                if gi == len(groups) - 1 and tb >= 1:
                    pl = [(tb - 1) * 4 + q_ for q_ in range(4)]
                    if c == 0:
                        if tb >= 2:
                            D2dve(pl[0] - 2)
                            D2dve(pl[0] - 1)
                        for t_ in pl:
                            D1a(t_)
                    elif c == 1:
                        for t_ in pl:
                            D1b(t_)
                        D2act(pl[0])
                        D2act(pl[1])
                    elif c == 2:
                        D2dve(pl[0])
                        D2dve(pl[1])
                        D2act(pl[2])
                        D2act(pl[3])
            for t4 in range(4):
                tile = tb * 4 + t4
                for dh in range(2):
                    ob = oi[0] % 2
                    oi[0] += 1
                    for c in range(n):
                        P.op("pe", lambda e, c=c, ob=ob, hs=hs, t4=t4, dh=dh, s_=s_: e.matmul(OB[ob], hT[hs][:, c, t4 * 128:(t4 + 1) * 128],
                                                                                          Ws[s_][:, c, dh * 512:(dh + 1) * 512],
                                                                                          start=(c == 0), stop=(c == n - 1)),
                             reads=[("hT", hs), ("W", s_)], writes=[("OB", ob)])
                    if False:
                        P.op("act", lambda e, ob=ob: e.activation(otmp[ob][:], OB[ob], AF.Copy), reads=[("OB", ob)], writes=[("otmp", ob)])
                        P.op("pool", lambda e, ob=ob, tile=tile, dh=dh: e.tensor_tensor(acc[:, tile, dh * 512:(dh + 1) * 512],
                                                                                       acc[:, tile, dh * 512:(dh + 1) * 512], otmp[ob][:], ALU.add),
                             reads=[("otmp", ob), ("acc", tile)], writes=[("acc", tile)])
                    else:
                        P.op("dve", lambda e, ob=ob, tile=tile, dh=dh: e.tensor_tensor(acc[:, tile, dh * 512:(dh + 1) * 512],
                                                                                      acc[:, tile, dh * 512:(dh + 1) * 512], OB[ob], ALU.add),
                             reads=[("OB", ob), ("acc", tile)], writes=[("acc", tile)])
            if gi == len(groups) - 1 and tb == 3:
                tl = [tb * 4 + q_ for q_ in range(4)]
                D2dve(tl[0] - 2)
                D2dve(tl[0] - 1)
                for t_ in tl:
                    D1a(t_)
                for t_ in tl:
                    D1b(t_)
                D2act(tl[0])
                D2act(tl[1])
                D2dve(tl[0])
                D2act(tl[2])
                D2dve(tl[1])
                D2act(tl[3])
                D2dve(tl[2])
                D2dve(tl[3])
        if gi + 2 < len(groups):
            load_ffn(gi + 2)

    P.finalize(sems)
    return nc


def _perm_tokens(p):
    own = np.concatenate([np.arange((2 * j + p) * 256, (2 * j + p + 1) * 256) for j in range(8)])
    oth = np.concatenate([np.arange((2 * j + 1 - p) * 256, (2 * j + 2 - p) * 256) for j in range(8)])
    return own, oth


def _const_tables(p):
    own, oth = _perm_tokens(p)
    pos = np.concatenate([own, oth]).astype(np.float64)
    inv = 500000.0 ** (-np.arange(0, 16, 2, dtype=np.float64) / 16.0)
    ang = pos[:, None] * inv[None, :]
    cos, sin = np.cos(ang), np.sin(ang)
    tab = np.concatenate([cos, cos, -sin, sin], axis=1).astype(np.float32)
    rope = tab.reshape(32, 128, 32).transpose(1, 0, 2).reshape(128, 32 * 32).copy()
    k = np.arange(128)[:, None, None]
    kt = np.arange(2)[None, :, None]
    q = np.arange(256)[None, None, :]
    cmask = np.where(q >= kt * 128 + k, 0.0, NEG).astype(np.float32).reshape(128, 512)
    ident = np.eye(128, dtype=np.float32)
    omask = np.full((128, 1), 0.0 if p == 1 else NEG, np.float32)
    bm = np.zeros((2, 16, 16), np.float32)
    os_ = np.zeros((2, 16, 16), np.float32)
    for qt in range(16):
        j = qt // 2
        for n in range(16):
            if n < 8:
                past = n < j
            else:
                i = n - 8
                past = (i < j) if p == 0 else (i <= j)
            bm[:, qt, n] = 0.0 if past else NEG
        os_[:, qt, j] = 1.0
    bmask = np.broadcast_to(bm.reshape(1, 512), (128, 512)).copy()
    osel = np.broadcast_to(os_.reshape(1, 512), (128, 512)).copy()
    onehot = np.zeros((16, S), np.float32)
    for n in range(16):
        onehot[n, n * 256:(n + 1) * 256] = 1.0
    return dict(rope=rope, cmask=cmask, ident=ident, omask=omask, bmask=bmask, osel=osel, onehot=onehot)


def _prep_inputs(x, w_in, diff_lambda, diff_subln_g, w_o, ln1_g, ln1_b, w_ffn_in, w_ffn_out, ln2_g, ln2_b):
    x = np.asarray(x, np.float32)
    w_in = np.asarray(w_in, np.float32)[0]
    dq, dk, dv = w_in[:, 0:512], w_in[:, 512:1024], w_in[:, 1024:1536]
    mq, mk, mv = w_in[:, 1536:2048], w_in[:, 2048:2560], w_in[:, 2560:3072]
    units = []
    for h in range(4):
        sl = slice(h * 128, (h + 1) * 128)
        units.append(np.concatenate([dq[:, sl], dk[:, sl], dv[:, sl]], axis=1))
    for m in range(4):
        sl = slice(m * 128, (m + 1) * 128)
        units.append(np.concatenate([mq[:, sl], mk[:, sl], mv[:, sl]], axis=1))
    wu = np.ascontiguousarray(np.stack(units, 0))
    wffn = np.asarray(w_ffn_in, np.float32)[0]
    shared = dict(
        wu=wu,
        wo=np.ascontiguousarray(np.asarray(w_o, np.float32)[0]),
        wg=np.ascontiguousarray(wffn[:, :DFF]),
        wup=np.ascontiguousarray(wffn[:, DFF:]),
        wout=np.ascontiguousarray(np.asarray(w_ffn_out, np.float32)[0]),
        lnp=np.ascontiguousarray(np.stack([np.broadcast_to(np.asarray(a, np.float32)[0][None, :], (128, D))
                                           for a in (ln1_g, ln1_b, ln2_g, ln2_b)], 0)),
        lam=np.ascontiguousarray(np.broadcast_to(np.asarray(diff_lambda, np.float32)[0].reshape(1, 256), (128, 256))),
        subg=np.ascontiguousarray(np.broadcast_to(np.asarray(diff_subln_g, np.float32)[0][None, :], (128, 128))),
    )
    in_maps = []
    consts = [_const_tables(0), _const_tables(1)]
    for c in range(8):
        b, p = c // 2, c % 2
        own, oth = _perm_tokens(p)
        perm = np.concatenate([own, oth])
        m = dict(shared)
        m.update(consts[p])
        m["xT"] = np.ascontiguousarray(x[b][perm].T)
        m["xown"] = np.ascontiguousarray(x[b][own])
        in_maps.append(m)
    return in_maps


_NC_CACHE = {}


def kernel(x, w_in, diff_lambda, diff_subln_g, w_o, ln1_g, ln1_b, w_ffn_in, w_ffn_out, ln2_g, ln2_b, _stage="full"):
    in_maps = _prep_inputs(x, w_in, diff_lambda, diff_subln_g, w_o, ln1_g, ln1_b, w_ffn_in, w_ffn_out, ln2_g, ln2_b)
    nc = build_nc(_stage)
    res = run_bass_kernel_spmd(nc, in_maps, core_ids=list(range(8)))
    key = "out" if _stage == "full" else "dbg"
    out = np.zeros((NB, S, D), np.float32)
    for c in range(8):
        b, p = c // 2, c % 2
        own, _ = _perm_tokens(p)
        out[b, own] = np.asarray(res.results[c][key], np.float32)
    return out
```
